# Optimizing a Trainium2 kernel written in Bass

```python
import math
import jax, jax.numpy as jnp
from jax import lax
import numpy as np

D_MODEL = 1024
BATCH = 8
SEQ = 2048
DEPTH = 4

N_MIXERS = 2
EXPAND = 2
D_INNER = EXPAND * D_MODEL
SSD_HEAD_DIM = 64
SSD_HEADS = D_INNER // SSD_HEAD_DIM
SSD_GROUPS = 8
SSD_STATE = 128
SSD_CONV = 5
SSD_CHUNK = 128
CONV_DIM = D_INNER + 2 * SSD_GROUPS * SSD_STATE
SSD_IN = D_INNER + CONV_DIM + 2 * SSD_HEADS
POOL_WINDOWS = (2, 4, 8, 16)
POOL_GROUPS = len(POOL_WINDOWS)
POOL_GROUP_DIM = D_INNER // POOL_GROUPS
POOL_IN = 2 * D_INNER
EPS = 1e-6
DT_MIN = 1e-3
DT_MAX = 1e-1

kernel_name = "bidir_ssd_pool_hybrid"


def rmsnorm(x, w):
    xf = x.astype(jnp.float32)
    y = xf * lax.rsqrt(jnp.mean(xf * xf, axis=-1, keepdims=True) + EPS)
    return (y * w.astype(jnp.float32)).astype(x.dtype)


def segsum(a):
    t = a.shape[-1]
    cs = jnp.cumsum(a, axis=-1)
    diff = cs[..., :, None] - cs[..., None, :]
    mask = jnp.tril(jnp.ones((t, t), dtype=bool))
    return jnp.where(mask, diff, -jnp.inf)


def ssd_scan(xs, a, bm, cm):
    b, l, h, p = xs.shape
    g, n = bm.shape[2], bm.shape[3]
    r = h // g
    q = SSD_CHUNK
    c = l // q
    xc = xs.reshape(b, c, q, g, r, p)
    ac = a.reshape(b, c, q, g, r).transpose(0, 3, 4, 1, 2)
    bc = bm.reshape(b, c, q, g, n)
    cc = cm.reshape(b, c, q, g, n)
    a_cs = jnp.cumsum(ac, axis=-1)
    lmat = jnp.exp(segsum(ac))
    cb = jnp.einsum('bclgn,bcsgn->bgcls', cc, bc)
    y_diag = jnp.einsum('bgrcls,bcsgrp->bclgrp', cb[:, :, None] * lmat, xc)
    decay_states = jnp.exp(a_cs[..., -1:] - a_cs)
    states = jnp.einsum('bclgn,bgrcl,bclgrp->bcgrpn', bc, decay_states, xc)
    chunk_decay = jnp.exp(a_cs[..., -1])

    def step(hstate, inp):
        s, dcy = inp
        return dcy[..., None, None] * hstate + s, hstate

    h0 = jnp.zeros((b, g, r, p, n), dtype=xs.dtype)
    _, prev = lax.scan(step, h0, (states.transpose(1, 0, 2, 3, 4, 5),
                                  chunk_decay.transpose(3, 0, 1, 2)))
    prev = prev.transpose(1, 0, 2, 3, 4, 5)
    y_off = jnp.einsum('bclgn,bcgrpn,bgrcl->bclgrp', cc, prev, jnp.exp(a_cs))
    return (y_diag + y_off).reshape(b, l, h, p)


def depthwise_conv_centred(x, w, bias):
    ch = x.shape[-1]
    k = w.shape[0]
    y = lax.conv_general_dilated(x, w[:, None, :].astype(x.dtype), window_strides=(1,),
                                 padding=[(k // 2, k // 2)],
                                 dimension_numbers=('NWC', 'WIO', 'NWC'),
                                 feature_group_count=ch)
    return y + bias.astype(x.dtype)


def ssd_mixer(u, w_in, conv_w, conv_b, dt_bias, a_log, d_skip, norm_w, w_out):
    b, l, _ = u.shape
    proj = u @ w_in
    z = proj[..., :D_INNER]
    xbc = proj[..., D_INNER:D_INNER + CONV_DIM]
    dt_raw = proj[..., D_INNER + CONV_DIM:]
    xbc = jax.nn.silu(depthwise_conv_centred(xbc, conv_w, conv_b))
    gn = SSD_GROUPS * SSD_STATE
    xs = xbc[..., :D_INNER].astype(jnp.float32).reshape(b, l, SSD_HEADS, SSD_HEAD_DIM)
    bm = xbc[..., D_INNER:D_INNER + gn].astype(jnp.float32).reshape(b, l, SSD_GROUPS, SSD_STATE)
    cm = xbc[..., D_INNER + gn:].astype(jnp.float32).reshape(b, l, SSD_GROUPS, SSD_STATE)
    dt = jax.nn.softplus(dt_raw.astype(jnp.float32).reshape(b, l, 2, SSD_HEADS)
                         + dt_bias.astype(jnp.float32))
    a = -jnp.exp(a_log.astype(jnp.float32))
    y = xs * d_skip.astype(jnp.float32)[:, None]
    for direction in range(2):
        xd = xs * dt[:, :, direction, :, None]
        ad = dt[:, :, direction] * a[direction]
        if direction == 1:
            yd = ssd_scan(jnp.flip(xd, 1), jnp.flip(ad, 1), jnp.flip(bm, 1), jnp.flip(cm, 1))
            yd = jnp.flip(yd, 1)
        else:
            yd = ssd_scan(xd, ad, bm, cm)
        y = y + yd
    y = y.reshape(b, l, D_INNER) * jax.nn.silu(z.astype(jnp.float32))
    yg = y.reshape(b, l, SSD_GROUPS, D_INNER // SSD_GROUPS)
    yg = yg * lax.rsqrt(jnp.mean(yg * yg, axis=-1, keepdims=True) + EPS)
    y = (yg.reshape(b, l, D_INNER) * norm_w.astype(jnp.float32)).astype(u.dtype)
    return y @ w_out


def pool_mixer(u, w_in, mix_w, scale, w_out):
    b, l, _ = u.shape
    proj = u @ w_in
    v = proj[..., :D_INNER]
    gate = proj[..., D_INNER:]
    vf = v.astype(jnp.float32)
    cs = jnp.concatenate([jnp.zeros((b, 1, D_INNER), jnp.float32),
                          jnp.cumsum(vf, axis=1)], axis=1)
    pos = jnp.arange(l)
    outs = []
    for gi, w in enumerate(POOL_WINDOWS):
        lo = jnp.clip(pos - w // 2, 0, l)
        hi = jnp.clip(pos + w - w // 2, 0, l)
        seg = cs[:, :, gi * POOL_GROUP_DIM:(gi + 1) * POOL_GROUP_DIM]
        s = jnp.take(seg, hi, axis=1) - jnp.take(seg, lo, axis=1)
        cnt = (hi - lo).astype(jnp.float32)
        outs.append(s / cnt[None, :, None])
    pooled = jnp.concatenate(outs, axis=-1) - vf
    pooled = pooled.reshape(b, l, POOL_GROUPS, POOL_GROUP_DIM)
    mixed = jnp.einsum('blgc,gcd->blgd', pooled, mix_w.astype(jnp.float32)).reshape(b, l, D_INNER)
    y = mixed * scale.astype(jnp.float32) * jax.nn.silu(gate.astype(jnp.float32))
    return y.astype(u.dtype) @ w_out


def setup_inputs(seed: int = 0) -> dict:
    key = jax.random.key(seed)
    ks = jax.random.split(key, 20)
    n_ssd = len(range(0, DEPTH, N_MIXERS))
    n_pool = DEPTH - n_ssd
    f32 = jnp.float32
    x = jax.random.normal(ks[0], (BATCH, SEQ, D_MODEL), f32)
    norm_w = 1.0 + 0.02 * jax.random.normal(ks[1], (DEPTH, D_MODEL), f32)
    ssd_w_in = jax.random.normal(ks[2], (n_ssd, D_MODEL, SSD_IN), f32) * D_MODEL ** -0.5
    ssd_conv_w = jax.random.normal(ks[3], (n_ssd, SSD_CONV, CONV_DIM), f32) * SSD_CONV ** -0.5
    ssd_conv_b = 0.02 * jax.random.normal(ks[4], (n_ssd, CONV_DIM), f32)
    dt0 = jnp.exp(jax.random.uniform(ks[5], (n_ssd, 2, SSD_HEADS), f32,
                                     math.log(DT_MIN), math.log(DT_MAX)))
    ssd_dt_bias = dt0 + jnp.log(-jnp.expm1(-dt0))
    ssd_a_log = jnp.log(jax.random.uniform(ks[6], (n_ssd, 2, SSD_HEADS), f32, 1.0, 16.0))
    ssd_d = 1.0 + 0.1 * jax.random.normal(ks[7], (n_ssd, SSD_HEADS), f32)
    ssd_norm_w = 1.0 + 0.02 * jax.random.normal(ks[8], (n_ssd, D_INNER), f32)
    ssd_w_out = jax.random.normal(ks[9], (n_ssd, D_INNER, D_MODEL), f32) * D_INNER ** -0.5
    pool_w_in = jax.random.normal(ks[10], (n_pool, D_MODEL, POOL_IN), f32) * D_MODEL ** -0.5
    pool_mix_w = jax.random.normal(ks[11], (n_pool, POOL_GROUPS, POOL_GROUP_DIM, POOL_GROUP_DIM),
                                   f32) * POOL_GROUP_DIM ** -0.5
    pool_scale = 1.0 + 0.1 * jax.random.normal(ks[12], (n_pool, D_INNER), f32)
    pool_w_out = jax.random.normal(ks[13], (n_pool, D_INNER, D_MODEL), f32) * D_INNER ** -0.5
    final_norm_w = 1.0 + 0.02 * jax.random.normal(ks[14], (D_MODEL,), f32)
    return {"x": x, "norm_w": norm_w,
            "ssd_w_in": ssd_w_in, "ssd_conv_w": ssd_conv_w, "ssd_conv_b": ssd_conv_b,
            "ssd_dt_bias": ssd_dt_bias, "ssd_a_log": ssd_a_log, "ssd_d": ssd_d,
            "ssd_norm_w": ssd_norm_w, "ssd_w_out": ssd_w_out,
            "pool_w_in": pool_w_in, "pool_mix_w": pool_mix_w, "pool_scale": pool_scale,
            "pool_w_out": pool_w_out, "final_norm_w": final_norm_w}


def reference(x, norm_w, ssd_w_in, ssd_conv_w, ssd_conv_b, ssd_dt_bias, ssd_a_log, ssd_d,
              ssd_norm_w, ssd_w_out, pool_w_in, pool_mix_w, pool_scale, pool_w_out,
              final_norm_w):
    h = x
    for i in range(DEPTH):
        u = rmsnorm(h, norm_w[i])
        j = i // N_MIXERS
        if i % N_MIXERS == 0:
            out = ssd_mixer(u, ssd_w_in[j], ssd_conv_w[j], ssd_conv_b[j], ssd_dt_bias[j],
                            ssd_a_log[j], ssd_d[j], ssd_norm_w[j], ssd_w_out[j])
        else:
            out = pool_mixer(u, pool_w_in[j], pool_mix_w[j], pool_scale[j], pool_w_out[j])
        h = h + out
    return rmsnorm(h, final_norm_w)
```

```python
from contextlib import ExitStack
import numpy as np
import ml_dtypes
import concourse.bass as bass
import concourse.mybir as mybir
from concourse.bass_utils import run_bass_kernel_spmd

F32 = mybir.dt.float32
BF16 = mybir.dt.bfloat16
AF = mybir.ActivationFunctionType
ALU = mybir.AluOpType
AX = mybir.AxisListType

D_MODEL = 1024
SEQ = 2048
NT = SEQ // 128
KC = D_MODEL // 128
D_INNER = 2048
EPS = 1e-6
POOL_WINDOWS = (2, 4, 8, 16)
NEG = -30000.0

ENGS = ("pe", "act", "dve", "pool", "sp")


class Prog:
    def __init__(self, nc, es):
        self.nc = nc
        self.es = es
        self.ops = {e: [] for e in ENGS}
        self.cnt = {e: 0 for e in ENGS}
        self.sem = {e: es.enter_context(nc.semaphore("s_" + e)) for e in ENGS}
        self.waited = {e: {} for e in ENGS}
        self.lastw = {}
        self.readers = {}
        self.dsem = {}
        self.nwaits = 0
        self.uid = 0
        self.bank_last = [dict() for _ in range(8)]

    def sb(self, name, shape, dt, es=None):
        self.uid += 1
        return (es or self.es).enter_context(
            self.nc.sbuf_tensor("%s_%d" % (name, self.uid), list(shape), dt))

    def ps(self, name, shape, dt=F32):
        return self.es.enter_context(self.nc.psum_tensor(name, list(shape), dt))

    def _deps(self, e, reads, writes, banks=()):
        deps = set()
        for b in banks:
            for e2, t in self.bank_last[b].items():
                if e2 != e:
                    deps.add(t)
        for k in reads:
            t = self.lastw.get(k)
            if t is not None:
                deps.add(t)
        for k in writes:
            t = self.lastw.get(k)
            if t is not None:
                deps.add(t)
            for t in self.readers.get(k, ()):
                deps.add(t)
        waits = []
        w = self.waited[e]
        for (sk, v) in deps:
            if sk == e and e == "pe":
                continue
            if w.get(sk, 0) < v:
                w[sk] = v
                waits.append((sk, v))
        self.nwaits += len(waits)
        return waits

    def _commit(self, tok, reads, writes):
        for k in reads:
            self.readers.setdefault(k, []).append(tok)
        for k in writes:
            self.lastw[k] = tok
            self.readers[k] = []

    def op(self, e, fn, reads=(), writes=(), banks=()):
        waits = self._deps(e, reads, writes, banks)
        self.cnt[e] += 1
        tok = (e, self.cnt[e])
        self.ops[e].append((waits, fn, (e, 1)))
        self._commit(tok, reads, writes)
        for b in banks:
            self.bank_last[b][e] = tok
        return tok

    def dma(self, q, fn, reads=(), writes=(), skey=None):
        if skey is None:
            skey = writes[0] if writes else reads[0]
        waits = self._deps(q, reads, writes)
        if skey not in self.dsem:
            self.dsem[skey] = [self.es.enter_context(self.nc.semaphore("d_%d" % len(self.dsem))), 0]
        ent = self.dsem[skey]
        ent[1] += 16
        sk = ("d", skey)
        tok = (sk, ent[1])
        self.ops[q].append((waits, fn, (sk, 16)))
        self._commit(tok, reads, writes)
        return tok

    def _semh(self, sk):
        if isinstance(sk, tuple):
            return self.dsem[sk[1]][0]
        return self.sem[sk]

    def barrier(self):
        toks = [(e, self.cnt[e]) for e in ENGS if self.cnt[e] > 0]
        toks += [(("d", k), v[1]) for k, v in self.dsem.items()]
        for e in ENGS:
            waits = []
            for (sk, v) in toks:
                if sk == e:
                    continue
                if self.waited[e].get(sk, 0) < v:
                    self.waited[e][sk] = v
                    waits.append((sk, v))
            if waits:
                self.ops[e].append((waits, None, None))

    def emit(self):
        nc = self.nc
        with nc.Block() as block:
            def run(eng, name):
                for (waits, fn, inc) in self.ops[name]:
                    for (sk, v) in waits:
                        eng.wait_ge(self._semh(sk), v)
                    if fn is None:
                        continue
                    ins = fn(eng)
                    ins.then_inc(self._semh(inc[0]), inc[1])

            @block.tensor
            def _(eng):
                run(eng, "pe")

            @block.scalar
            def _(eng):
                run(eng, "act")

            @block.vector
            def _(eng):
                run(eng, "dve")

            @block.gpsimd
            def _(eng):
                run(eng, "pool")

            @block.sync
            def _(eng):
                run(eng, "sp")


def MM(P, out, lhsT, rhs, start, stop, reads, writes, bk=()):
    P.op("pe", lambda e: e.matmul(out, lhsT=lhsT, rhs=rhs, start=start, stop=stop), reads, writes, bk)


def TR(P, out, in_, ident, reads, writes, bk=()):
    P.op("pe", lambda e: e.transpose(out=out, in_=in_, identity=ident), reads, writes, bk)


def ACT(P, out, in_, func, reads, writes, bias=None, scale=1.0, bk=()):
    if bias is None:
        P.op("act", lambda e: e.activation(out=out, in_=in_, func=func, scale=scale), reads, writes, bk)
    else:
        P.op("act", lambda e: e.activation(out=out, in_=in_, func=func, bias=bias, scale=scale),
             reads, writes, bk)


def TT(P, out, in0, in1, op, reads, writes, eng="dve", bk=()):
    P.op(eng, lambda e: e.tensor_tensor(out=out, in0=in0, in1=in1, op=op), reads, writes, bk)


def TS(P, out, in0, s1, s2, op0, op1, reads, writes, eng="dve"):
    if s2 is None:
        P.op(eng, lambda e: e.tensor_scalar(out=out, in0=in0, scalar1=s1, scalar2=None, op0=op0),
             reads, writes)
    else:
        P.op(eng, lambda e: e.tensor_scalar(out=out, in0=in0, scalar1=s1, scalar2=s2, op0=op0, op1=op1),
             reads, writes)


def STT(P, out, in0, scalar, in1, op0, op1, reads, writes, bk=()):
    P.op("dve", lambda e: e.scalar_tensor_tensor(out=out, in0=in0, scalar=scalar, in1=in1,
                                                  op0=op0, op1=op1), reads, writes, bk)


def CP(P, out, in_, reads, writes, eng="dve", bk=()):
    if eng == "act":
        P.op("act", lambda e: e.copy(out=out, in_=in_), reads, writes, bk)
    else:
        P.op(eng, lambda e: e.tensor_copy(out=out, in_=in_), reads, writes, bk)


def DMA(P, q, out, in_, reads, writes, skey=None):
    return P.dma(q, lambda e: e.dma_start(out=out, in_=in_), reads, writes, skey)


def _bf16_split(a):
    hi = a.astype(ml_dtypes.bfloat16)
    lo = (a - hi.astype(np.float32)).astype(ml_dtypes.bfloat16)
    return hi, lo


def _pool_bands():
    mats = []
    idx = {}
    L = SEQ
    for gi, w in enumerate(POOL_WINDOWS):
        t = np.arange(L)
        lo = np.clip(t - w // 2, 0, L)
        hi = np.clip(t + w - w // 2, 0, L)
        A = np.zeros((L, L), np.float64)
        for tt in range(L):
            A[lo[tt]:hi[tt], tt] = 1.0 / (hi[tt] - lo[tt])
        A -= np.eye(L)
        A = A.astype(np.float32)
        blocks = {
            "sub": A[0:128, 128:256], "main": A[128:256, 128:256], "sup": A[256:384, 128:256],
            "first": A[0:128, 0:128], "last": A[L - 128:L, L - 128:L],
        }
        for name, blk in blocks.items():
            ids = []
            h_, l_ = _bf16_split(np.ascontiguousarray(blk))
            for m in (h_, l_):
                if np.any(m.astype(np.float32) != 0):
                    ids.append(len(mats))
                    mats.append(m)
            idx[(gi, name)] = ids
    arr = np.stack(mats, axis=1)
    return np.ascontiguousarray(arr), idx


def _ssd_consts():
    k = np.arange(128)
    uincl = (k[:, None] <= k[None, :]).astype(np.float32)
    uinclT = uincl.T.copy()
    ident = np.eye(128, dtype=np.float32)
    nmf = NEG * (k[None, :] < k[:, None]).astype(np.float32)
    nmb = NEG * (k[None, :] > k[:, None]).astype(np.float32)
    ones = np.ones((128, 128), np.float32)
    arr = np.stack([ident, uincl, uinclT, nmf, nmb, ones], axis=1).astype(ml_dtypes.bfloat16)
    return np.ascontiguousarray(arr)


C_ID, C_UI, C_UIT, C_NMF, C_NMB, C_ONE = range(6)


class Ctx:
    pass


def build_program(layers=(0, 1, 2, 3), final_norm=True):
    nc = bass.Bass("TRN2", target_bir_lowering=False)
    bands_np, band_idx = _pool_bands()
    NB = bands_np.shape[1]

    def din(name, shape, dt=F32):
        return nc.dram_tensor(name, list(shape), dt, kind="ExternalInput").ap()

    D = Ctx()
    D.x = din("x", [SEQ, D_MODEL])
    D.norm_w = din("norm_w", [4, D_MODEL])
    D.final_w = din("final_norm_w", [1, D_MODEL])
    D.ssd_wA = din("ssd_wA", [2, 8, 128, 4096])
    D.ssd_wz = din("ssd_wz", [2, 8, 128, 2048])
    D.ssd_wo = din("ssd_wo", [2, 8, 128, 2048])
    D.ssd_wdt = din("ssd_wdt", [2, 128, 512])
    D.ssd_convw = din("ssd_convw", [2, 128, 32 * 5])
    D.ssd_convb = din("ssd_convb", [2, 128, 32])
    D.ssd_dt_bias = din("ssd_dt_bias", [2, 64])
    D.ssd_a_log = din("ssd_a_log", [2, 64])
    D.ssd_d = din("ssd_d", [2, 32])
    D.ssd_norm_w = din("ssd_norm_w", [2, D_INNER])
    D.pool_wvg = din("pool_wvg", [2, 4, 2, 128, 4096])
    D.pool_mix = din("pool_mix", [2, 4, 128, 2048])
    D.pool_scale = din("pool_scale", [2, 128, 16])
    D.pool_wo = din("pool_wo", [2, 4, 128, 4096])
    D.cst = din("cst", [128, 6 * 128], BF16)
    D.bands = din("bands", [128, NB * 128], BF16)
    D.out = nc.dram_tensor("out", [SEQ, D_MODEL], F32, kind="ExternalOutput").ap()

    with ExitStack() as es:
        P = Prog(nc, es)
        G = Ctx()
        G.h = P.sb("h", [128, NT, D_MODEL], F32)
        G.uT = P.sb("uT", [128, KC, SEQ], BF16)
        G.cst = P.sb("cst", [128, 6, 128], BF16)
        G.nwb = P.sb("nwb", [128, D_MODEL], F32)
        G.sq = P.sb("sq", [128, D_MODEL], F32)
        G.ss = P.sb("ss", [128, NT], F32)
        G.rstd = P.sb("rstd", [128, NT], F32)
        G.ubf = [P.sb("ubf%d" % i, [128, D_MODEL], BF16) for i in range(2)]
        G.pbig = P.ps("pbig", [128, 2048], F32)
        G.p4 = P.ps("p4", [128, 512], F32)
        G.p5 = P.ps("p5", [128, 512], F32)
        G.p6 = P.ps("p6", [128, 512], F32)
        G.p7 = P.ps("p7", [128, 1024], BF16)
        G.ident = G.cst[:, C_ID, :]

        DMA(P, "sp", G.cst[:], D.cst.rearrange("p (a b) -> p a b", a=6), [], ["cst"])
        xv = D.x.rearrange("(i p) d -> p i d", p=128)
        for i in range(0, NT, 4):
            DMA(P, "sp", G.h[:, i:i + 4, :], xv[:, i:i + 4, :], [], [("h", j) for j in range(i, i + 4)],
                skey=("hld", i))

        first = True
        for L in layers:
            if not first:
                P.barrier()
            first = False
            with ExitStack() as les:
                emit_norm(P, G, D.norm_w[L:L + 1, :], tag="n%d" % L)
                if L % 2 == 0:
                    emit_ssd(P, G, D, L // 2, les)
                else:
                    emit_pool(P, G, D, L // 2, les, band_idx, NB)
                P.barrier()
        toks = []
        if final_norm:
            toks = emit_final(P, G, D)
        else:
            ov = D.out.rearrange("(i p) d -> p i d", p=128)
            for i in range(0, NT, 4):
                toks.append(DMA(P, "sp", ov[:, i:i + 4, :], G.h[:, i:i + 4, :],
                                [("h", j) for j in range(i, i + 4)], [], skey=("ost", i)))
        P.ops["sp"].append(([(sk, v) for (sk, v) in toks], None, None))
        P.emit()
        stats = dict(cnt=dict(P.cnt), waits=P.nwaits)
    return nc, stats


def emit_stats(P, G, tag):
    for i in range(NT):
        ACT(P, G.sq[:], G.h[:, i, :], AF.Square, [("h", i)], ["sq"])
        P.op("dve", lambda e, i=i: e.tensor_reduce(out=G.ss[:, i:i + 1], in_=G.sq[:], axis=AX.X, op=ALU.add),
             ["sq"], [("ss", i)])
    allss = [("ss", i) for i in range(NT)]
    TS(P, G.ss[:], G.ss[:], 1.0 / D_MODEL, EPS, ALU.mult, ALU.add, allss, allss)
    ACT(P, G.ss[:], G.ss[:], AF.Sqrt, allss, allss)
    P.op("dve", lambda e: e.reciprocal(out=G.rstd[:], in_=G.ss[:]), allss, ["rstd"])


def emit_norm(P, G, nw_row, tag):
    DMA(P, "sp", G.nwb[:], nw_row.partition_broadcast(128), [], ["nwb"])
    emit_stats(P, G, tag)
    for i in range(NT):
        ub = G.ubf[i % 2]
        uk = ("ubf", i % 2)
        STT(P, ub[:], G.h[:, i, :], G.rstd[:, i:i + 1], G.nwb[:], ALU.mult, ALU.mult,
            [("h", i), "rstd", "nwb"], [uk])
        pt, pb_ = (G.p7[:], 7) if i % 2 == 0 else (G.p6[:].bitcast(BF16), 6)
        for kc in range(KC):
            TR(P, pt[:, kc * 128:(kc + 1) * 128], ub[:, kc * 128:(kc + 1) * 128], G.ident,
               [uk, "cst"], [], bk=[pb_])
        CP(P, G.uT[:, :, i * 128:(i + 1) * 128], pt.rearrange("p (k t) -> p k t", k=KC),
           [], [("uT", i)], eng=("act" if i % 2 == 0 else "dve"), bk=[pb_])


def emit_final(P, G, D):
    DMA(P, "sp", G.nwb[:], D.final_w.partition_broadcast(128), [], ["nwb"])
    emit_stats(P, G, "fin")
    toks = []
    ov = D.out.rearrange("(i p) d -> p i d", p=128)
    for i in range(NT):
        STT(P, G.h[:, i, :], G.h[:, i, :], G.rstd[:, i:i + 1], G.nwb[:], ALU.mult, ALU.mult,
            [("h", i), "rstd", "nwb"], [("h", i)])
        if i % 4 == 3:
            toks.append(DMA(P, "sp", ov[:, i - 3:i + 1, :], G.h[:, i - 3:i + 1, :],
                            [("h", j) for j in range(i - 3, i + 1)], [], skey=("ost", i)))
    return toks


def uT_keys(tb):
    return [("uT", i) for i in range(tb * 4, tb * 4 + 4)]


def emit_pool(P, G, D, j, les, band_idx, NB):
    S = Ctx()
    S.bands = P.sb("bands", [128, NB, 128], BF16, les)
    S.scale = P.sb("pscale", [128, 16], F32, les)
    S.wv = P.sb("wv", [128, KC, 512], BF16, les)
    S.wg = P.sb("wg", [128, KC, 512], BF16, les)
    S.mx = P.sb("mx", [128, 4, 512], BF16, les)
    S.wo = P.sb("wo", [128, 4, D_MODEL], BF16, les)
    S.v = P.sb("v", [128, NT, 512], BF16, les)
    S.pT = P.sb("pT", [128, 4, 512], BF16, les)
    S.sg = [P.sb("sg%d" % i, [128, 512], F32, les) for i in range(2)]
    S.yT = P.sb("yT", [128, 4, 512], BF16, les)
    lt = "pl%d" % j
    DMA(P, "sp", S.bands[:], D.bands.rearrange("p (a b) -> p a b", a=NB), [], [lt + "bands"])
    DMA(P, "sp", S.scale[:], D.pool_scale[j], [], [lt + "scale"])
    pbanks = [G.pbig[:, b * 512:(b + 1) * 512] for b in range(4)]
    c2 = lambda ap: ap.rearrange("p (n e) -> p n e", e=2048)
    f2 = lambda t: t[:].rearrange("p a b -> p (a b)").rearrange("p (n e) -> p n e", e=2048)
    for g in range(4):
        DMA(P, "pool", f2(S.wv), c2(D.pool_wvg[j, g, 0]), [], [lt + "wv"])
        DMA(P, "pool", f2(S.wg), c2(D.pool_wvg[j, g, 1]), [], [lt + "wg"])
        DMA(P, "pool", f2(S.mx), c2(D.pool_mix[j, g]), [], [lt + "mx"])
        DMA(P, "pool", f2(S.wo), c2(D.pool_wo[j, g]), [], [lt + "wo"])
        for i in range(NT):
            pb = pbanks[i % 4]
            pk = ("pbig", i % 4)
            for kc in range(KC):
                MM(P, pb, G.uT[:, kc, i * 128:(i + 1) * 128], S.wv[:, kc, :], kc == 0, kc == KC - 1,
                   [("uT", i), lt + "wv"], [], bk=[i % 4])
            CP(P, S.v[:, i, :], pb, [], [(lt + "v", i)], eng=("act" if i % 2 == 0 else "dve"), bk=[i % 4])
        for tb in range(4):
            for cb in range(4):
                pb = pbanks[cb]
                pk = ("pbig", cb)
                for jj in range(4):
                    jt = tb * 4 + jj
                    terms = []
                    if jt > 0:
                        terms += [(jt - 1, m) for m in band_idx[(g, "sub")]]
                    nm = "first" if jt == 0 else ("last" if jt == NT - 1 else "main")
                    terms += [(jt, m) for m in band_idx[(g, nm)]]
                    if jt < NT - 1:
                        terms += [(jt + 1, m) for m in band_idx[(g, "sup")]]
                    for n, (it, m) in enumerate(terms):
                        MM(P, pb[:, jj * 128:(jj + 1) * 128], S.v[:, it, cb * 128:(cb + 1) * 128],
                           S.bands[:, m, :], n == 0, n == len(terms) - 1,
                           [(lt + "v", it), lt + "bands"], [], bk=[cb])
                CP(P, S.pT[:, cb, :], pb, [], [(lt + "pT", cb)], eng=("act" if cb % 2 == 0 else "dve"), bk=[cb])
            for db in range(4):
                sg = S.sg[db % 2]
                sgk = (lt + "sg", db % 2)
                MMk = ("p4",)
                for kc in range(KC):
                    MM(P, G.p4[:], S.wg[:, kc, db * 128:(db + 1) * 128], G.uT[:, kc, tb * 512:(tb + 1) * 512],
                       kc == 0, kc == KC - 1, uT_keys(tb) + [lt + "wg"], [], bk=[4])
                ACT(P, sg[:], G.p4[:], AF.Silu, [], [sgk], bk=[4])
                pm = G.p5 if db % 2 == 0 else G.p6
                pmk = 5 if db % 2 == 0 else 6
                for cb in range(4):
                    MM(P, pm[:], S.mx[:, cb, db * 128:(db + 1) * 128], S.pT[:, cb, :], cb == 0, cb == 3,
                       [(lt + "pT", cb), lt + "mx"], [], bk=[pmk])
                sc = S.scale[:, g * 4 + db:g * 4 + db + 1]
                STT(P, S.yT[:, db, :], pm[:], sc, sg[:], ALU.mult, ALU.mult,
                    [sgk, lt + "scale"], [(lt + "yT", db)], bk=[pmk])
            for jj in range(4):
                it = tb * 4 + jj
                for ch in range(2):
                    po = G.p5 if ch == 0 else G.p6
                    pok = 5 if ch == 0 else 6
                    for db in range(4):
                        MM(P, po[:], S.yT[:, db, jj * 128:(jj + 1) * 128], S.wo[:, db, ch * 512:(ch + 1) * 512],
                           db == 0, db == 3, [(lt + "yT", db), lt + "wo"], [], bk=[pok])
                    hs = G.h[:, it, ch * 512:(ch + 1) * 512]
                    TT(P, hs, po[:], hs, ALU.add, [("h", it)], [("h", it)], bk=[pok])


import os as _os
_NG = int(_os.environ.get("SSD_NG", "8"))


def emit_ssd(P, G, D, j, les):
    lt = "sd%d" % j
    S = Ctx()
    S.convw = P.sb("convw", [128, 32, 5], F32, les)
    S.convb = P.sb("convb", [128, 32], F32, les)
    S.dtb = P.sb("dtb", [128, 64], F32, les)
    S.Abc = P.sb("Abc", [128, 64], F32, les)
    S.Dt = P.sb("Dt", [128, 32], F32, les)
    S.wdt = P.sb("wdt", [128, KC, 64], BF16, les)
    S.a_all = P.sb("a_all", [128, NT, 64], F32, les)
    S.lndt = P.sb("lndt", [128, NT, 64], F32, les)
    S.tA = P.sb("tA", [128, NT, 64], F32, les)
    S.tB = G.sq[:].rearrange("p (c k) -> p c k", k=64)
    S.wz = P.sb("wz", [128, KC, 256], BF16, les)
    S.wA = P.sb("wA", [128, 4096], BF16, les)
    S.wx = S.wA[:, 0:2048].rearrange("p (k c) -> p k c", k=KC)
    S.wB = S.wA[:, 2048:3072].rearrange("p (k c) -> p k c", k=KC)
    S.wC = S.wA[:, 3072:4096].rearrange("p (k c) -> p k c", k=KC)
    S.wo = P.sb("wo", [128, 2, D_MODEL], BF16, les)
    S.gnw = P.sb("gnw", [128, 256], F32, les)
    S.acc = P.sb("acc", [128, SEQ], F32, les)
    S.xcT = P.sb("xcT", [128, SEQ], BF16, les)
    S.BT = P.sb("BT", [128, SEQ], BF16, les)
    S.CT = P.sb("CT", [128, SEQ], BF16, les)
    S.xt = P.sb("xt", [128, NT, 256], BF16, les)
    S.Bt = P.sb("Bt", [128, NT, 128], BF16, les)
    S.Gs = P.sb("Gs", [128, NT, 256], BF16, les)
    S.ahi = P.sb("ahi", [128, NT, 8], BF16, les)
    S.alo = P.sb("alo", [128, NT, 8], BF16, les)
    S.nb = P.sb("nb", [128, NT, 8], F32, les)
    S.ecs = P.sb("ecs", [128, NT, 8], F32, les)
    S.dsc = P.sb("dsc", [128, NT, 8], F32, les)
    S.edec = P.sb("edec", [128, NT, 8], F32, les)
    S.tg = P.sb("tg", [128, NT, 8], F32, les)
    S.GS = P.sb("GS", [128, 256], F32, les)
    S.SS = P.sb("SS", [128, 256], F32, les)
    S.Lexp = [P.sb("Lexp%d" % i, [128, 128], F32, les) for i in range(4)]
    S.MT = [P.sb("MT%d" % i, [128, 128], BF16, les) for i in range(4)]
    S.xdd = [[P.sb("xdd%d_%d" % (d, i), [128, 256], BF16, les) for i in range(2)] for d in range(2)]
    S.sz = [P.sb("sz%d" % i, [128, 256], F32, les) for i in range(2)]
    S.t1 = [P.sb("t1_%d" % i, [128, 256], F32, les) for i in range(2)]
    S.t2 = [P.sb("t2_%d" % i, [128, 256], F32, les) for i in range(2)]
    S.t3 = [P.sb("t3_%d" % i, [128, 256], F32, les) for i in range(2)]
    S.y = [P.sb("y_%d" % i, [128, 256], F32, les) for i in range(4)]
    S.sqy = [P.sb("sqy%d" % i, [128, 256], F32, les) for i in range(2)]
    S.ssq = [P.sb("ssq%d" % i, [128, 1], F32, les) for i in range(4)]
    S.yn = [P.sb("yn%d" % i, [128, 256], BF16, les) for i in range(2)]
    S.ynT = [P.sb("ynT%d" % i, [128, 2, 128], BF16, les) for i in range(2)]

    cst = G.cst
    ident = G.ident
    c2 = lambda ap: ap.rearrange("p (n e) -> p n e", e=2048)
    pbank = [G.pbig[:, b * 512:(b + 1) * 512] for b in range(4)]
    p6b = G.p6[:].bitcast(BF16)
    p7f = G.p7[:].bitcast(F32)
    ALLB = [0, 1, 2, 3]

    DMA(P, "sp", S.convw[:], D.ssd_convw[j].rearrange("p (c t) -> p c t", t=5), [], [lt + "convw"])
    DMA(P, "sp", S.convb[:], D.ssd_convb[j], [], [lt + "convb"])
    DMA(P, "sp", S.dtb[:], D.ssd_dt_bias[j:j + 1, :].partition_broadcast(128), [], [lt + "dtb"])
    DMA(P, "sp", S.Abc[:], D.ssd_a_log[j:j + 1, :].partition_broadcast(128), [], [lt + "Abc"])
    DMA(P, "sp", S.Dt[:], D.ssd_d[j:j + 1, :].partition_broadcast(128), [], [lt + "Dt"])
    DMA(P, "pool", S.wdt[:].rearrange("p a b -> p (a b)"), D.ssd_wdt[j], [], [lt + "wdt"])
    ACT(P, S.Abc[:], S.Abc[:], AF.Exp, [lt + "Abc"], [lt + "Abc"])
    TS(P, S.Abc[:], S.Abc[:], -1.0, None, ALU.mult, None, [lt + "Abc"], [lt + "Abc"])

    for c in range(NT):
        bkc = 4 + (c % 2)
        pd = (G.p4 if c % 2 == 0 else G.p5)[:, 0:64]
        for kc in range(KC):
            MM(P, pd, G.uT[:, kc, c * 128:(c + 1) * 128], S.wdt[:, kc, :], kc == 0, kc == KC - 1,
               [("uT", c), lt + "wdt"], [], bk=[bkc])
        TT(P, S.tA[:, c, :], pd, S.dtb[:], ALU.add, [lt + "dtb"], [(lt + "tA", c)], bk=[bkc])
    kA, kB, kl, ka = lt + "tA", "sq", lt + "lndt", lt + "a_all"
    allA = [(lt + "tA", c) for c in range(NT)]
    TS(P, S.tB[:], S.tA[:], -1.0, None, ALU.mult, None, allA, [kB])
    TT(P, S.tB[:], S.tB[:], S.tA[:], ALU.min, allA + [kB], [kB])
    ACT(P, S.tB[:], S.tB[:], AF.Exp, [kB], [kB])
    TS(P, S.tB[:], S.tB[:], 1.0, None, ALU.add, None, [kB], [kB])
    ACT(P, S.tB[:], S.tB[:], AF.Ln, [kB], [kB])
    TS(P, S.tA[:], S.tA[:], 0.0, None, ALU.max, None, allA, allA)
    TT(P, S.tA[:], S.tA[:], S.tB[:], ALU.add, allA + [kB], allA)
    ACT(P, S.lndt[:], S.tA[:], AF.Ln, allA, [kl])
    TT(P, S.a_all[:], S.tA[:], S.Abc[:].unsqueeze(1).to_broadcast([128, NT, 64]), ALU.mult,
       allA + [lt + "Abc"], [ka])

    b4 = lambda ap: ap.unsqueeze(2).to_broadcast([128, 4, 64])
    v64 = lambda ap: ap.rearrange("p (h e) -> p h e", h=4)
    v4 = lambda t: t[:, :, :].rearrange("p c (d h) -> p c d h", d=2)

    for g in range(_NG):
        gt = lt
        DMA(P, "pool", c2(S.wA[:]), c2(D.ssd_wA[j, g]), [], [gt + "wA"])
        DMA(P, "pool", S.wz[:].rearrange("p a b -> p (a b)"), D.ssd_wz[j, g], [], [gt + "wz"])
        DMA(P, "pool", S.wo[:].rearrange("p a b -> p (a b)"), D.ssd_wo[j, g], [], [gt + "wo"])
        DMA(P, "sp", S.gnw[:], D.ssd_norm_w[j:j + 1, g * 256:(g + 1) * 256].partition_broadcast(128),
            [], [gt + "gnw"])
        a_g = S.a_all[:, :, :].rearrange("p c (d g h) -> p c d g h", d=2, g=8)[:, :, :, g, :]
        l_g = S.lndt[:, :, :].rearrange("p c (d g h) -> p c d g h", d=2, g=8)[:, :, :, g, :]
        CP(P, v4(S.ahi), a_g, [ka], [gt + "ahi"])
        TT(P, v4(S.alo), a_g, v4(S.ahi), ALU.subtract, [ka, gt + "ahi"], [gt + "alo"])
        pcs = G.p5[:, 0:128].rearrange("p (c k) -> p c k", k=8)
        ptot = G.p5[:, 128:256].rearrange("p (c k) -> p c k", k=8)
        for c in range(NT):
            for (dst, lo, hi, ci) in ((pcs, 0, 4, C_UI), (pcs, 4, 8, C_UIT), (ptot, 0, 8, C_ONE)):
                MM(P, dst[:, c, lo:hi], cst[:, ci, :], S.ahi[:, c, lo:hi], True, False,
                   [gt + "ahi", "cst"], [], bk=[5])
                MM(P, dst[:, c, lo:hi], cst[:, ci, :], S.alo[:, c, lo:hi], False, True,
                   [gt + "alo", "cst"], [], bk=[5])
        TT(P, v4(S.nb), l_g, v4(pcs), ALU.subtract, [kl], [gt + "nb"], bk=[5])
        ACT(P, S.ecs[:], pcs, AF.Exp, [], [gt + "ecs"], bk=[5])
        ACT(P, S.edec[:], ptot, AF.Exp, [], [gt + "edec"], bk=[5])
        TT(P, S.tg[:], ptot, S.nb[:], ALU.add, [gt + "nb"], [gt + "tg"], bk=[5])
        ACT(P, S.dsc[:], S.tg[:], AF.Exp, [gt + "tg"], [gt + "dsc"])

        tiles = [(S.wx[:, :, 0:128], gt + "wA", g * 2, S.xcT, "x", 0),
                 (S.wx[:, :, 128:256], gt + "wA", g * 2 + 1, S.xcT, "x", 1),
                 (S.wB, gt + "wA", 16 + g, S.BT, "B", 0),
                 (S.wC, gt + "wA", 24 + g, S.CT, "C", 0)]
        for (wt, wk, ct, dst, kind, sub) in tiles:
            for tb in range(4):
                for kc in range(KC):
                    MM(P, pbank[tb], wt[:, kc, :], G.uT[:, kc, tb * 512:(tb + 1) * 512], kc == 0, kc == KC - 1,
                       uT_keys(tb) + [wk], [], bk=[tb])
            ACT(P, S.acc[:], G.pbig[:], AF.Identity, [lt + "convw", lt + "convb"], [gt + "acc"],
                bias=S.convb[:, ct:ct + 1], scale=S.convw[:, ct, 2:3], bk=ALLB)
            for tap in (1, 3, 0, 4):
                s = tap - 2
                t0, t1 = max(0, -s), SEQ - max(0, s)
                STT(P, S.acc[:, t0:t1], G.pbig[:, t0 + s:t1 + s], S.convw[:, ct, tap:tap + 1], S.acc[:, t0:t1],
                    ALU.mult, ALU.add, [gt + "acc", lt + "convw"], [gt + "acc"], bk=ALLB)
            dk = gt + "T" + kind
            ACT(P, dst[:], S.acc[:], AF.Silu, [gt + "acc"], [dk])
            if kind == "C":
                continue
            for half in range(2):
                pt = G.p7[:] if half == 0 else p6b
                pbk = [7] if half == 0 else [6]
                for ii in range(8):
                    i = half * 8 + ii
                    TR(P, pt[:, ii * 128:(ii + 1) * 128], dst[:, i * 128:(i + 1) * 128], ident, [dk, "cst"], [], bk=pbk)
                src = pt.rearrange("p (i c) -> p i c", i=8)
                if kind == "x":
                    CP(P, S.xt[:, half * 8:half * 8 + 8, sub * 128:(sub + 1) * 128], src, [],
                       [(gt + "xt", i) for i in range(half * 8, half * 8 + 8)],
                       eng=("act" if half == 0 else "dve"), bk=pbk)
                else:
                    CP(P, S.Bt[:, half * 8:half * 8 + 8, :], src, [],
                       [(gt + "Bt", i) for i in range(half * 8, half * 8 + 8)],
                       eng=("act" if half == 0 else "dve"), bk=pbk)

        Ss = S.acc[:].bitcast(BF16).rearrange("p (c e) -> p c e", c=NT)
        ak = gt + "acc"
        P.op("dve", lambda e: e.memset(S.GS[:], 0.0), [], [gt + "GS"])
        P.op("dve", lambda e: e.memset(S.SS[:], 0.0), [], [gt + "SS"])
        P.op("dve", lambda e: e.memset(S.Gs[:, NT - 1, :], 0.0), [], [(gt + "Gs", NT - 1)])
        P.op("dve", lambda e: e.memset(Ss[:, 0, :], 0.0), [], [ak])

        def st_mm(c, d, i):
            xd = S.xdd[d][i % 2]
            xk = (gt + "xdd", d, i % 2)
            bkn = (4 + (i % 2)) if d == 0 else (6 + (i % 2))
            bank = [G.p4[:], G.p5[:], G.p6[:], p7f][bkn - 4]
            TT(P, v64(xd[:]), v64(S.xt[:, c, :]), b4(S.dsc[:, c, d * 4:d * 4 + 4]), ALU.mult,
               [(gt + "xt", c), gt + "dsc"], [xk], eng=("dve" if d == 0 else "pool"))
            MM(P, bank[:, 0:256], S.Bt[:, c, :], xd[:], True, True, [(gt + "Bt", c), xk], [], bk=[bkn])
            return bank[:, 0:256], bkn

        pend = {}
        pend[(0, 0)] = st_mm(0, 0, 0)
        pend[(1, 0)] = st_mm(NT - 1, 1, 0)
        for i in range(NT - 1):
            cf, cb = i, NT - 1 - i
            if i + 1 < NT - 1:
                pend[(0, i + 1)] = st_mm(cf + 1, 0, i + 1)
                pend[(1, i + 1)] = st_mm(cb - 1, 1, i + 1)
            pst, bkn = pend.pop((0, i))
            TT(P, v64(S.SS[:]), v64(S.SS[:]), b4(S.edec[:, cf, 0:4]), ALU.mult, [gt + "SS", gt + "edec"], [gt + "SS"])
            TT(P, S.SS[:], pst, S.SS[:], ALU.add, [gt + "SS"], [gt + "SS"], bk=[bkn])
            CP(P, Ss[:, cf + 1, :], S.SS[:], [gt + "SS"], [ak], eng="act")
            pst, bkn = pend.pop((1, i))
            TT(P, v64(S.GS[:]), v64(S.GS[:]), b4(S.edec[:, cb, 4:8]), ALU.mult, [gt + "GS", gt + "edec"], [gt + "GS"])
            TT(P, S.GS[:], pst, S.GS[:], ALU.add, [gt + "GS"], [gt + "GS"], bk=[bkn])
            CP(P, S.Gs[:, cb - 1, :], S.GS[:], [gt + "GS"], [(gt + "Gs", cb - 1)], eng="act")

        def tail(c):
            par = c % 2
            yn, ynT = S.yn[par], S.ynT[par]
            ptr = G.p7[:, 0:256]
            for jj in range(2):
                TR(P, ptr[:, jj * 128:(jj + 1) * 128], yn[:, jj * 128:(jj + 1) * 128], ident,
                   [(gt + "yn", par), "cst"], [], bk=[7])
            CP(P, ynT[:], ptr.rearrange("p (j t) -> p j t", j=2), [], [(gt + "ynT", par)], eng="act", bk=[7])
            for ch in range(2):
                po = G.p6[:] if ch == 0 else p7f
                pok = [6] if ch == 0 else [7]
                for jj in range(2):
                    MM(P, po, ynT[:, jj, :], S.wo[:, jj, ch * 512:(ch + 1) * 512], jj == 0, jj == 1,
                       [(gt + "ynT", par), gt + "wo"], [], bk=pok)
                hs = G.h[:, c, ch * 512:(ch + 1) * 512]
                TT(P, hs, po, hs, ALU.add, [("h", c)], [("h", c)], bk=pok)

        for c in range(NT + 3):
          if c < NT:
            par = c % 2
            tok = slice(c * 128, (c + 1) * 128)
            pcb = pbank[2][:, 0:128]
            MM(P, pcb, S.BT[:, tok], S.CT[:, tok], True, True, [gt + "TB", gt + "TC"], [], bk=[2])
            pyo = G.p4
            MM(P, pyo[:, 0:256], S.CT[:, tok], Ss[:, c, :], True, True, [gt + "TC", ak], [], bk=[4])
            MM(P, pyo[:, 256:512], S.CT[:, tok], S.Gs[:, c, :], True, True, [gt + "TC", (gt + "Gs", c)], [], bk=[4])
            pz = G.p5[:, 0:256]
            for kc in range(KC):
                MM(P, pz, G.uT[:, kc, tok], S.wz[:, kc, :], kc == 0, kc == KC - 1, [("uT", c), gt + "wz"], [], bk=[5])
            sz = S.sz[par]
            ACT(P, sz[:], pz, AF.Silu, [], [(gt + "sz", par)], bk=[5])
            py = pbank[3][:, 0:256]
            its = [(hh, d) for hh in range(4) for d in range(2)]

            def lps(n):
                hh, d = its[n]
                col = d * 4 + hh
                lb = n % 2
                pl = pbank[lb][:, 0:128]
                ucst = cst[:, C_UI if d == 0 else C_UIT, :]
                mcst = cst[:, C_NMF if d == 0 else C_NMB, :]
                MM(P, pl, S.ahi[:, c, col:col + 1].to_broadcast([128, 128]), ucst, True, False,
                   [gt + "ahi", "cst"], [], bk=[lb])
                MM(P, pl, S.alo[:, c, col:col + 1].to_broadcast([128, 128]), ucst, False, False,
                   [gt + "alo", "cst"], [], bk=[lb])
                MM(P, pl, ident, mcst, False, True, ["cst"], [], bk=[lb])

            lps(0)
            lps(1)
            for n in range(8):
                hh, d = its[n]
                col = d * 4 + hh
                sl = n % 4
                lb = n % 2
                pl = pbank[lb][:, 0:128]
                Lx = S.Lexp[sl]
                ACT(P, Lx[:], pl, AF.Exp, [gt + "nb"], [(gt + "Lexp", sl)], bias=S.nb[:, c, col:col + 1], bk=[lb])
                MTt = S.MT[sl]
                TT(P, MTt[:], pcb, Lx[:], ALU.mult, [(gt + "Lexp", sl)], [(gt + "MT", sl)], bk=[2])
                if n + 2 < 8:
                    lps(n + 2)
                MM(P, py[:, hh * 64:(hh + 1) * 64], MTt[:], S.xt[:, c, hh * 64:(hh + 1) * 64], d == 0, d == 1,
                   [(gt + "MT", sl), (gt + "xt", c)], [], bk=[3])
            t1, t2, t3, y = S.t1[par], S.t2[par], S.t3[par], S.y[c % 4]
            k1, k2, k3, ky = (gt + "t1", par), (gt + "t2", par), (gt + "t3", par), (gt + "y", c % 4)
            TT(P, v64(t1[:]), v64(pyo[:, 0:256]), b4(S.ecs[:, c, 0:4]), ALU.mult, [gt + "ecs"], [k1], bk=[4])
            TT(P, v64(t2[:]), v64(pyo[:, 256:512]), b4(S.ecs[:, c, 4:8]), ALU.mult, [gt + "ecs"], [k2], bk=[4])
            TT(P, y[:], py, t1[:], ALU.add, [k1], [ky], bk=[3])
            TT(P, v64(t3[:]), v64(S.xt[:, c, :]), b4(S.Dt[:, g * 4:g * 4 + 4]), ALU.mult,
               [(gt + "xt", c), lt + "Dt"], [k3], eng="pool")
            TT(P, t2[:], t2[:], t3[:], ALU.add, [k2, k3], [k2], eng="pool")
            TT(P, y[:], y[:], t2[:], ALU.add, [ky, k2], [ky], eng="pool")
            TT(P, y[:], y[:], sz[:], ALU.mult, [ky, (gt + "sz", par)], [ky], eng="pool")
            TT(P, S.sqy[par][:], y[:], y[:], ALU.mult, [ky], [(gt + "sqy", par)], eng="pool")
          if 1 <= c <= NT:
            cc = c - 1
            ssq = S.ssq[cc % 4]
            sk = (gt + "ssq", cc % 4)
            P.op("dve", lambda e, ssq=ssq, q=S.sqy[cc % 2]: e.tensor_reduce(out=ssq[:], in_=q[:], axis=AX.X, op=ALU.add),
                 [(gt + "sqy", cc % 2)], [sk])
            TS(P, ssq[:], ssq[:], 1.0 / 256.0, EPS, ALU.mult, ALU.add, [sk], [sk])
            ACT(P, ssq[:], ssq[:], AF.Sqrt, [sk], [sk])
          if 2 <= c <= NT + 1:
            cc = c - 2
            ssq = S.ssq[cc % 4]
            sk = (gt + "ssq", cc % 4)
            P.op("dve", lambda e, ssq=ssq: e.reciprocal(out=ssq[:], in_=ssq[:]), [sk], [sk])
            yn = S.yn[cc % 2]
            STT(P, yn[:], S.y[cc % 4][:], ssq[:, 0:1], S.gnw[:], ALU.mult, ALU.mult,
                [(gt + "y", cc % 4), sk, gt + "gnw"], [(gt + "yn", cc % 2)])
          if c >= 3:
            tail(c - 3)


_CACHE = {}


def _host_inputs(inp):
    f32 = np.float32
    bands_np, _ = _pool_bands()
    convw = np.asarray(inp["ssd_conv_w"], f32)
    convw_l = np.ascontiguousarray(convw.reshape(2, 5, 32, 128).transpose(0, 3, 2, 1)).reshape(2, 128, 160)
    convb_l = np.ascontiguousarray(np.asarray(inp["ssd_conv_b"], f32).reshape(2, 32, 128).transpose(0, 2, 1))
    pscale_l = np.ascontiguousarray(np.asarray(inp["pool_scale"], f32).reshape(2, 16, 128).transpose(0, 2, 1))
    W = np.asarray(inp["ssd_w_in"], f32).reshape(2, KC, 128, 6208)
    wxp = W[..., 2048:4096].reshape(2, KC, 128, 8, 256).transpose(0, 3, 2, 1, 4).reshape(2, 8, 128, 2048)
    wBp = W[..., 4096:5120].reshape(2, KC, 128, 8, 128).transpose(0, 3, 2, 1, 4).reshape(2, 8, 128, 1024)
    wCp = W[..., 5120:6144].reshape(2, KC, 128, 8, 128).transpose(0, 3, 2, 1, 4).reshape(2, 8, 128, 1024)
    wA = np.ascontiguousarray(np.concatenate([wxp, wBp, wCp], axis=-1))
    wz = np.ascontiguousarray(W[..., 0:2048].reshape(2, KC, 128, 8, 256).transpose(0, 3, 2, 1, 4).reshape(2, 8, 128, 2048))
    wdt = np.ascontiguousarray(W[..., 6144:6208].transpose(0, 2, 1, 3).reshape(2, 128, 512))
    wso = np.ascontiguousarray(np.asarray(inp["ssd_w_out"], f32).reshape(2, 8, 2, 128, 1024)
                               .transpose(0, 1, 3, 2, 4).reshape(2, 8, 128, 2048))
    PW = np.asarray(inp["pool_w_in"], f32).reshape(2, KC, 128, 2, 4, 512)
    pwvg = np.ascontiguousarray(PW.transpose(0, 4, 3, 2, 1, 5).reshape(2, 4, 2, 128, 4096))
    pmix = np.ascontiguousarray(np.asarray(inp["pool_mix_w"], f32).reshape(2, 4, 4, 128, 512)
                                .transpose(0, 1, 3, 2, 4).reshape(2, 4, 128, 2048))
    pwo = np.ascontiguousarray(np.asarray(inp["pool_w_out"], f32).reshape(2, 4, 4, 128, 1024)
                               .transpose(0, 1, 3, 2, 4).reshape(2, 4, 128, 4096))
    shared = {
        "norm_w": np.ascontiguousarray(inp["norm_w"], f32),
        "final_norm_w": np.ascontiguousarray(inp["final_norm_w"], f32).reshape(1, D_MODEL),
        "ssd_wA": wA, "ssd_wz": wz, "ssd_wo": wso, "ssd_wdt": wdt,
        "ssd_convw": convw_l,
        "ssd_convb": convb_l,
        "ssd_dt_bias": np.ascontiguousarray(inp["ssd_dt_bias"], f32).reshape(2, 64),
        "ssd_a_log": np.ascontiguousarray(inp["ssd_a_log"], f32).reshape(2, 64),
        "ssd_d": np.ascontiguousarray(inp["ssd_d"], f32),
        "ssd_norm_w": np.ascontiguousarray(inp["ssd_norm_w"], f32),
        "pool_wvg": pwvg, "pool_mix": pmix, "pool_wo": pwo,
        "pool_scale": pscale_l,
        "cst": _ssd_consts().reshape(128, 6 * 128),
        "bands": bands_np.reshape(128, -1),
    }
    return shared


def run_layers(inp, x, layers, final_norm):
    key = (tuple(layers), final_norm)
    if key not in _CACHE:
        _CACHE[key] = build_program(layers, final_norm)[0]
    nc = _CACHE[key]
    shared = _host_inputs(inp)
    B = x.shape[0]
    in_maps = []
    for b in range(B):
        m = dict(shared)
        m["x"] = np.ascontiguousarray(x[b], np.float32)
        in_maps.append(m)
    res = run_bass_kernel_spmd(nc, in_maps, core_ids=list(range(B)))
    return np.stack([np.asarray(r["out"], np.float32) for r in res.results], axis=0)


def kernel(**inputs):
    x = np.asarray(inputs["x"], np.float32)
    return run_layers(inputs, x, (0, 1, 2, 3), True)
```

```python
from contextlib import ExitStack
import numpy as np
import ml_dtypes
import concourse.bass as bass
import concourse.mybir as mybir
from concourse.bass_utils import run_bass_kernel_spmd

F32 = mybir.dt.float32
BF16 = mybir.dt.bfloat16
AF = mybir.ActivationFunctionType
ALU = mybir.AluOpType
AX = mybir.AxisListType

D_MODEL = 1024
SEQ = 2048
NT = SEQ // 128
KC = D_MODEL // 128
D_INNER = 2048
EPS = 1e-6
POOL_WINDOWS = (2, 4, 8, 16)
NEG = -30000.0

ENGS = ("pe", "act", "dve", "pool", "sp")


class Prog:
    def __init__(self, nc, es):
        self.nc = nc
        self.es = es
        self.ops = {e: [] for e in ENGS}
        self.cnt = {e: 0 for e in ENGS}
        self.sem = {e: es.enter_context(nc.semaphore("s_" + e)) for e in ENGS}
        self.waited = {e: {} for e in ENGS}
        self.lastw = {}
        self.readers = {}
        self.dsem = {}
        self.nwaits = 0
        self.uid = 0
        self.bank_last = [dict() for _ in range(8)]

    def sb(self, name, shape, dt, es=None):
        self.uid += 1
        return (es or self.es).enter_context(
            self.nc.sbuf_tensor("%s_%d" % (name, self.uid), list(shape), dt))

    def ps(self, name, shape, dt=F32):
        return self.es.enter_context(self.nc.psum_tensor(name, list(shape), dt))

    def _deps(self, e, reads, writes, banks=()):
        deps = set()
        for b in banks:
            for e2, t in self.bank_last[b].items():
                if e2 != e:
                    deps.add(t)
        for k in reads:
            t = self.lastw.get(k)
            if t is not None:
                deps.add(t)
        for k in writes:
            t = self.lastw.get(k)
            if t is not None:
                deps.add(t)
            for t in self.readers.get(k, ()):
                deps.add(t)
        waits = []
        w = self.waited[e]
        for (sk, v) in deps:
            if sk == e and e == "pe":
                continue
            if w.get(sk, 0) < v:
                w[sk] = v
                waits.append((sk, v))
        self.nwaits += len(waits)
        return waits

    def _commit(self, tok, reads, writes):
        for k in reads:
            self.readers.setdefault(k, []).append(tok)
        for k in writes:
            self.lastw[k] = tok
            self.readers[k] = []

    def op(self, e, fn, reads=(), writes=(), banks=()):
        waits = self._deps(e, reads, writes, banks)
        self.cnt[e] += 1
        tok = (e, self.cnt[e])
        self.ops[e].append((waits, fn, (e, 1)))
        self._commit(tok, reads, writes)
        for b in banks:
            self.bank_last[b][e] = tok
        return tok

    def dma(self, q, fn, reads=(), writes=(), skey=None):
        if skey is None:
            skey = writes[0] if writes else reads[0]
        waits = self._deps(q, reads, writes)
        if skey not in self.dsem:
            self.dsem[skey] = [self.es.enter_context(self.nc.semaphore("d_%d" % len(self.dsem))), 0]
        ent = self.dsem[skey]
        ent[1] += 16
        sk = ("d", skey)
        tok = (sk, ent[1])
        self.ops[q].append((waits, fn, (sk, 16)))
        self._commit(tok, reads, writes)
        return tok

    def _semh(self, sk):
        if isinstance(sk, tuple):
            return self.dsem[sk[1]][0]
        return self.sem[sk]

    def barrier(self):
        toks = [(e, self.cnt[e]) for e in ENGS if self.cnt[e] > 0]
        toks += [(("d", k), v[1]) for k, v in self.dsem.items()]
        for e in ENGS:
            waits = []
            for (sk, v) in toks:
                if sk == e:
                    continue
                if self.waited[e].get(sk, 0) < v:
                    self.waited[e][sk] = v
                    waits.append((sk, v))
            if waits:
                self.ops[e].append((waits, None, None))

    def emit(self):
        nc = self.nc
        with nc.Block() as block:
            def run(eng, name):
                for (waits, fn, inc) in self.ops[name]:
                    for (sk, v) in waits:
                        eng.wait_ge(self._semh(sk), v)
                    if fn is None:
                        continue
                    ins = fn(eng)
                    ins.then_inc(self._semh(inc[0]), inc[1])

            @block.tensor
            def _(eng):
                run(eng, "pe")

            @block.scalar
            def _(eng):
                run(eng, "act")

            @block.vector
            def _(eng):
                run(eng, "dve")

            @block.gpsimd
            def _(eng):
                run(eng, "pool")

            @block.sync
            def _(eng):
                run(eng, "sp")


def MM(P, out, lhsT, rhs, start, stop, reads, writes, bk=()):
    P.op("pe", lambda e: e.matmul(out, lhsT=lhsT, rhs=rhs, start=start, stop=stop), reads, writes, bk)


def TR(P, out, in_, ident, reads, writes, bk=()):
    P.op("pe", lambda e: e.transpose(out=out, in_=in_, identity=ident), reads, writes, bk)


def ACT(P, out, in_, func, reads, writes, bias=None, scale=1.0, bk=()):
    if bias is None:
        P.op("act", lambda e: e.activation(out=out, in_=in_, func=func, scale=scale), reads, writes, bk)
    else:
        P.op("act", lambda e: e.activation(out=out, in_=in_, func=func, bias=bias, scale=scale),
             reads, writes, bk)


def TT(P, out, in0, in1, op, reads, writes, eng="dve", bk=()):
    P.op(eng, lambda e: e.tensor_tensor(out=out, in0=in0, in1=in1, op=op), reads, writes, bk)


def TS(P, out, in0, s1, s2, op0, op1, reads, writes, eng="dve"):
    if s2 is None:
        P.op(eng, lambda e: e.tensor_scalar(out=out, in0=in0, scalar1=s1, scalar2=None, op0=op0),
             reads, writes)
    else:
        P.op(eng, lambda e: e.tensor_scalar(out=out, in0=in0, scalar1=s1, scalar2=s2, op0=op0, op1=op1),
             reads, writes)


def STT(P, out, in0, scalar, in1, op0, op1, reads, writes, bk=()):
    P.op("dve", lambda e: e.scalar_tensor_tensor(out=out, in0=in0, scalar=scalar, in1=in1,
                                                  op0=op0, op1=op1), reads, writes, bk)


def CP(P, out, in_, reads, writes, eng="dve", bk=()):
    if eng == "act":
        P.op("act", lambda e: e.copy(out=out, in_=in_), reads, writes, bk)
    else:
        P.op(eng, lambda e: e.tensor_copy(out=out, in_=in_), reads, writes, bk)


def DMA(P, q, out, in_, reads, writes, skey=None):
    return P.dma(q, lambda e: e.dma_start(out=out, in_=in_), reads, writes, skey)


def _bf16_split(a):
    hi = a.astype(ml_dtypes.bfloat16)
    lo = (a - hi.astype(np.float32)).astype(ml_dtypes.bfloat16)
    return hi, lo


def _pool_bands():
    mats = []
    idx = {}
    L = SEQ
    for gi, w in enumerate(POOL_WINDOWS):
        t = np.arange(L)
        lo = np.clip(t - w // 2, 0, L)
        hi = np.clip(t + w - w // 2, 0, L)
        A = np.zeros((L, L), np.float64)
        for tt in range(L):
            A[lo[tt]:hi[tt], tt] = 1.0 / (hi[tt] - lo[tt])
        A -= np.eye(L)
        A = A.astype(np.float32)
        blocks = {
            "sub": A[0:128, 128:256], "main": A[128:256, 128:256], "sup": A[256:384, 128:256],
            "first": A[0:128, 0:128], "last": A[L - 128:L, L - 128:L],
        }
        for name, blk in blocks.items():
            ids = []
            h_, l_ = _bf16_split(np.ascontiguousarray(blk))
            for m in (h_, l_):
                if np.any(m.astype(np.float32) != 0):
                    ids.append(len(mats))
                    mats.append(m)
            idx[(gi, name)] = ids
    arr = np.stack(mats, axis=1)
    return np.ascontiguousarray(arr), idx


def _ssd_consts():
    k = np.arange(128)
    uincl = (k[:, None] <= k[None, :]).astype(np.float32)
    uinclT = uincl.T.copy()
    ident = np.eye(128, dtype=np.float32)
    nmf = NEG * (k[None, :] < k[:, None]).astype(np.float32)
    nmb = NEG * (k[None, :] > k[:, None]).astype(np.float32)
    ones = np.ones((128, 128), np.float32)
    arr = np.stack([ident, uincl, uinclT, nmf, nmb, ones], axis=1).astype(ml_dtypes.bfloat16)
    return np.ascontiguousarray(arr)


C_ID, C_UI, C_UIT, C_NMF, C_NMB, C_ONE = range(6)


class Ctx:
    pass


def build_program(layers=(0, 1, 2, 3), final_norm=True):
    nc = bass.Bass("TRN2", target_bir_lowering=False)
    bands_np, band_idx = _pool_bands()
    NB = bands_np.shape[1]

    def din(name, shape, dt=F32):
        return nc.dram_tensor(name, list(shape), dt, kind="ExternalInput").ap()

    D = Ctx()
    D.x = din("x", [SEQ, D_MODEL])
    D.norm_w = din("norm_w", [4, D_MODEL])
    D.final_w = din("final_norm_w", [1, D_MODEL])
    D.ssd_wA = din("ssd_wA", [2, 8, 128, 4096])
    D.ssd_wz = din("ssd_wz", [2, 8, 128, 2048])
    D.ssd_wo = din("ssd_wo", [2, 8, 128, 2048])
    D.ssd_wdt = din("ssd_wdt", [2, 128, 512])
    D.ssd_convw = din("ssd_convw", [2, 128, 32 * 5])
    D.ssd_convb = din("ssd_convb", [2, 128, 32])
    D.ssd_dt_bias = din("ssd_dt_bias", [2, 64])
    D.ssd_a_log = din("ssd_a_log", [2, 64])
    D.ssd_d = din("ssd_d", [2, 32])
    D.ssd_norm_w = din("ssd_norm_w", [2, D_INNER])
    D.pool_wvg = din("pool_wvg", [2, 4, 2, 128, 4096])
    D.pool_mix = din("pool_mix", [2, 4, 128, 2048])
    D.pool_scale = din("pool_scale", [2, 128, 16])
    D.pool_wo = din("pool_wo", [2, 4, 128, 4096])
    D.cst = din("cst", [128, 6 * 128], BF16)
    D.bands = din("bands", [128, NB * 128], BF16)
    D.out = nc.dram_tensor("out", [SEQ, D_MODEL], F32, kind="ExternalOutput").ap()

    with ExitStack() as es:
        P = Prog(nc, es)
        G = Ctx()
        G.h = P.sb("h", [128, NT, D_MODEL], F32)
        G.uT = P.sb("uT", [128, KC, SEQ], BF16)
        G.cst = P.sb("cst", [128, 6, 128], BF16)
        G.nwb = P.sb("nwb", [128, D_MODEL], F32)
        G.sq = P.sb("sq", [128, D_MODEL], F32)
        G.ss = P.sb("ss", [128, NT], F32)
        G.rstd = P.sb("rstd", [128, NT], F32)
        G.ubf = [P.sb("ubf%d" % i, [128, D_MODEL], BF16) for i in range(2)]
        G.pbig = P.ps("pbig", [128, 2048], F32)
        G.p4 = P.ps("p4", [128, 512], F32)
        G.p5 = P.ps("p5", [128, 512], F32)
        G.p6 = P.ps("p6", [128, 512], F32)
        G.p7 = P.ps("p7", [128, 1024], BF16)
        G.ident = G.cst[:, C_ID, :]

        DMA(P, "sp", G.cst[:], D.cst.rearrange("p (a b) -> p a b", a=6), [], ["cst"])
        xv = D.x.rearrange("(i p) d -> p i d", p=128)
        for i in range(0, NT, 4):
            DMA(P, "sp", G.h[:, i:i + 4, :], xv[:, i:i + 4, :], [], [("h", j) for j in range(i, i + 4)],
                skey=("hld", i))

        first = True
        for L in layers:
            if not first:
                P.barrier()
            first = False
            with ExitStack() as les:
                emit_norm(P, G, D.norm_w[L:L + 1, :], tag="n%d" % L)
                if L % 2 == 0:
                    emit_ssd(P, G, D, L // 2, les)
                else:
                    emit_pool(P, G, D, L // 2, les, band_idx, NB)
                P.barrier()
        toks = []
        if final_norm:
            toks = emit_final(P, G, D)
        else:
            ov = D.out.rearrange("(i p) d -> p i d", p=128)
            for i in range(0, NT, 4):
                toks.append(DMA(P, "sp", ov[:, i:i + 4, :], G.h[:, i:i + 4, :],
                                [("h", j) for j in range(i, i + 4)], [], skey=("ost", i)))
        P.ops["sp"].append(([(sk, v) for (sk, v) in toks], None, None))
        P.emit()
        stats = dict(cnt=dict(P.cnt), waits=P.nwaits)
    return nc, stats


def emit_stats(P, G, tag):
    for i in range(NT):
        ACT(P, G.sq[:], G.h[:, i, :], AF.Square, [("h", i)], ["sq"])
        P.op("dve", lambda e, i=i: e.tensor_reduce(out=G.ss[:, i:i + 1], in_=G.sq[:], axis=AX.X, op=ALU.add),
             ["sq"], [("ss", i)])
    allss = [("ss", i) for i in range(NT)]
    TS(P, G.ss[:], G.ss[:], 1.0 / D_MODEL, EPS, ALU.mult, ALU.add, allss, allss)
    ACT(P, G.ss[:], G.ss[:], AF.Sqrt, allss, allss)
    P.op("dve", lambda e: e.reciprocal(out=G.rstd[:], in_=G.ss[:]), allss, ["rstd"])


def emit_norm(P, G, nw_row, tag):
    DMA(P, "sp", G.nwb[:], nw_row.partition_broadcast(128), [], ["nwb"])
    emit_stats(P, G, tag)
    for i in range(NT):
        ub = G.ubf[i % 2]
        uk = ("ubf", i % 2)
        STT(P, ub[:], G.h[:, i, :], G.rstd[:, i:i + 1], G.nwb[:], ALU.mult, ALU.mult,
            [("h", i), "rstd", "nwb"], [uk])
        pt, pb_ = (G.p7[:], 7) if i % 2 == 0 else (G.p6[:].bitcast(BF16), 6)
        for kc in range(KC):
            TR(P, pt[:, kc * 128:(kc + 1) * 128], ub[:, kc * 128:(kc + 1) * 128], G.ident,
               [uk, "cst"], [], bk=[pb_])
        CP(P, G.uT[:, :, i * 128:(i + 1) * 128], pt.rearrange("p (k t) -> p k t", k=KC),
           [], [("uT", i)], eng=("act" if i % 2 == 0 else "dve"), bk=[pb_])


def emit_final(P, G, D):
    DMA(P, "sp", G.nwb[:], D.final_w.partition_broadcast(128), [], ["nwb"])
    emit_stats(P, G, "fin")
    toks = []
    ov = D.out.rearrange("(i p) d -> p i d", p=128)
    for i in range(NT):
        STT(P, G.h[:, i, :], G.h[:, i, :], G.rstd[:, i:i + 1], G.nwb[:], ALU.mult, ALU.mult,
            [("h", i), "rstd", "nwb"], [("h", i)])
        if i % 4 == 3:
            toks.append(DMA(P, "sp", ov[:, i - 3:i + 1, :], G.h[:, i - 3:i + 1, :],
                            [("h", j) for j in range(i - 3, i + 1)], [], skey=("ost", i)))
    return toks


def uT_keys(tb):
    return [("uT", i) for i in range(tb * 4, tb * 4 + 4)]


def emit_pool(P, G, D, j, les, band_idx, NB):
    S = Ctx()
    S.bands = P.sb("bands", [128, NB, 128], BF16, les)
    S.scale = P.sb("pscale", [128, 16], F32, les)
    S.wv = P.sb("wv", [128, KC, 512], BF16, les)
    S.wg = P.sb("wg", [128, KC, 512], BF16, les)
    S.mx = P.sb("mx", [128, 4, 512], BF16, les)
    S.wo = P.sb("wo", [128, 4, D_MODEL], BF16, les)
    S.v = P.sb("v", [128, NT, 512], BF16, les)
    S.pT = P.sb("pT", [128, 4, 512], BF16, les)
    S.sg = [P.sb("sg%d" % i, [128, 512], F32, les) for i in range(2)]
    S.yT = P.sb("yT", [128, 4, 512], BF16, les)
    lt = "pl%d" % j
    DMA(P, "sp", S.bands[:], D.bands.rearrange("p (a b) -> p a b", a=NB), [], [lt + "bands"])
    DMA(P, "sp", S.scale[:], D.pool_scale[j], [], [lt + "scale"])
    pbanks = [G.pbig[:, b * 512:(b + 1) * 512] for b in range(4)]
    c2 = lambda ap: ap.rearrange("p (n e) -> p n e", e=2048)
    f2 = lambda t: t[:].rearrange("p a b -> p (a b)").rearrange("p (n e) -> p n e", e=2048)
    for g in range(4):
        DMA(P, "pool", f2(S.wv), c2(D.pool_wvg[j, g, 0]), [], [lt + "wv"])
        DMA(P, "pool", f2(S.wg), c2(D.pool_wvg[j, g, 1]), [], [lt + "wg"])
        DMA(P, "pool", f2(S.mx), c2(D.pool_mix[j, g]), [], [lt + "mx"])
        DMA(P, "pool", f2(S.wo), c2(D.pool_wo[j, g]), [], [lt + "wo"])
        for i in range(NT):
            pb = pbanks[i % 4]
            pk = ("pbig", i % 4)
            for kc in range(KC):
                MM(P, pb, G.uT[:, kc, i * 128:(i + 1) * 128], S.wv[:, kc, :], kc == 0, kc == KC - 1,
                   [("uT", i), lt + "wv"], [], bk=[i % 4])
            CP(P, S.v[:, i, :], pb, [], [(lt + "v", i)], eng=("act" if i % 2 == 0 else "dve"), bk=[i % 4])
        for tb in range(4):
            for cb in range(4):
                pb = pbanks[cb]
                pk = ("pbig", cb)
                for jj in range(4):
                    jt = tb * 4 + jj
                    terms = []
                    if jt > 0:
                        terms += [(jt - 1, m) for m in band_idx[(g, "sub")]]
                    nm = "first" if jt == 0 else ("last" if jt == NT - 1 else "main")
                    terms += [(jt, m) for m in band_idx[(g, nm)]]
                    if jt < NT - 1:
                        terms += [(jt + 1, m) for m in band_idx[(g, "sup")]]
                    for n, (it, m) in enumerate(terms):
                        MM(P, pb[:, jj * 128:(jj + 1) * 128], S.v[:, it, cb * 128:(cb + 1) * 128],
                           S.bands[:, m, :], n == 0, n == len(terms) - 1,
                           [(lt + "v", it), lt + "bands"], [], bk=[cb])
                CP(P, S.pT[:, cb, :], pb, [], [(lt + "pT", cb)], eng=("act" if cb % 2 == 0 else "dve"), bk=[cb])
            for db in range(4):
                sg = S.sg[db % 2]
                sgk = (lt + "sg", db % 2)
                MMk = ("p4",)
                for kc in range(KC):
                    MM(P, G.p4[:], S.wg[:, kc, db * 128:(db + 1) * 128], G.uT[:, kc, tb * 512:(tb + 1) * 512],
                       kc == 0, kc == KC - 1, uT_keys(tb) + [lt + "wg"], [], bk=[4])
                ACT(P, sg[:], G.p4[:], AF.Silu, [], [sgk], bk=[4])
                pm = G.p5 if db % 2 == 0 else G.p6
                pmk = 5 if db % 2 == 0 else 6
                for cb in range(4):
                    MM(P, pm[:], S.mx[:, cb, db * 128:(db + 1) * 128], S.pT[:, cb, :], cb == 0, cb == 3,
                       [(lt + "pT", cb), lt + "mx"], [], bk=[pmk])
                sc = S.scale[:, g * 4 + db:g * 4 + db + 1]
                STT(P, S.yT[:, db, :], pm[:], sc, sg[:], ALU.mult, ALU.mult,
                    [sgk, lt + "scale"], [(lt + "yT", db)], bk=[pmk])
            for jj in range(4):
                it = tb * 4 + jj
                for ch in range(2):
                    po = G.p5 if ch == 0 else G.p6
                    pok = 5 if ch == 0 else 6
                    for db in range(4):
                        MM(P, po[:], S.yT[:, db, jj * 128:(jj + 1) * 128], S.wo[:, db, ch * 512:(ch + 1) * 512],
                           db == 0, db == 3, [(lt + "yT", db), lt + "wo"], [], bk=[pok])
                    hs = G.h[:, it, ch * 512:(ch + 1) * 512]
                    TT(P, hs, po[:], hs, ALU.add, [("h", it)], [("h", it)], bk=[pok])


import os as _os
_NG = int(_os.environ.get("SSD_NG", "8"))


def emit_ssd(P, G, D, j, les):
    lt = "sd%d" % j
    S = Ctx()
    S.convw = P.sb("convw", [128, 32, 5], F32, les)
    S.convb = P.sb("convb", [128, 32], F32, les)
    S.dtb = P.sb("dtb", [128, 64], F32, les)
    S.Abc = P.sb("Abc", [128, 64], F32, les)
    S.Dt = P.sb("Dt", [128, 32], F32, les)
    S.wdt = P.sb("wdt", [128, KC, 64], BF16, les)
    S.a_all = P.sb("a_all", [128, NT, 64], F32, les)
    S.lndt = P.sb("lndt", [128, NT, 64], F32, les)
    S.tA = P.sb("tA", [128, NT, 64], F32, les)
    S.tB = G.sq[:].rearrange("p (c k) -> p c k", k=64)
    S.wz = P.sb("wz", [128, KC, 256], BF16, les)
    S.wA = P.sb("wA", [128, 4096], BF16, les)
    S.wx = S.wA[:, 0:2048].rearrange("p (k c) -> p k c", k=KC)
    S.wB = S.wA[:, 2048:3072].rearrange("p (k c) -> p k c", k=KC)
    S.wC = S.wA[:, 3072:4096].rearrange("p (k c) -> p k c", k=KC)
    S.wo = P.sb("wo", [128, 2, D_MODEL], BF16, les)
    S.gnw = P.sb("gnw", [128, 256], F32, les)
    S.acc = P.sb("acc", [128, SEQ], F32, les)
    S.xcT = P.sb("xcT", [128, SEQ], BF16, les)
    S.BT = P.sb("BT", [128, SEQ], BF16, les)
    S.CT = P.sb("CT", [128, SEQ], BF16, les)
    S.xt = P.sb("xt", [128, NT, 256], BF16, les)
    S.Bt = P.sb("Bt", [128, NT, 128], BF16, les)
    S.Gs = P.sb("Gs", [128, NT, 256], BF16, les)
    S.ahi = P.sb("ahi", [128, NT, 8], BF16, les)
    S.alo = P.sb("alo", [128, NT, 8], BF16, les)
    S.nb = P.sb("nb", [128, NT, 8], F32, les)
    S.ecs = P.sb("ecs", [128, NT, 8], F32, les)
    S.dsc = P.sb("dsc", [128, NT, 8], F32, les)
    S.edec = P.sb("edec", [128, NT, 8], F32, les)
    S.tg = P.sb("tg", [128, NT, 8], F32, les)
    S.GS = P.sb("GS", [128, 256], F32, les)
    S.SS = P.sb("SS", [128, 256], F32, les)
    S.Lexp = [P.sb("Lexp%d" % i, [128, 128], F32, les) for i in range(4)]
    S.MT = [P.sb("MT%d" % i, [128, 128], BF16, les) for i in range(4)]
    S.xdd = [[P.sb("xdd%d_%d" % (d, i), [128, 256], BF16, les) for i in range(2)] for d in range(2)]
    S.sz = [P.sb("sz%d" % i, [128, 256], F32, les) for i in range(2)]
    S.t1 = [P.sb("t1_%d" % i, [128, 256], F32, les) for i in range(2)]
    S.t2 = [P.sb("t2_%d" % i, [128, 256], F32, les) for i in range(2)]
    S.t3 = [P.sb("t3_%d" % i, [128, 256], F32, les) for i in range(2)]
    S.y = [P.sb("y_%d" % i, [128, 256], F32, les) for i in range(4)]
    S.sqy = [P.sb("sqy%d" % i, [128, 256], F32, les) for i in range(2)]
    S.ssq = [P.sb("ssq%d" % i, [128, 1], F32, les) for i in range(4)]
    S.yn = [P.sb("yn%d" % i, [128, 256], BF16, les) for i in range(2)]
    S.ynT = [P.sb("ynT%d" % i, [128, 2, 128], BF16, les) for i in range(2)]

    cst = G.cst
    ident = G.ident
    c2 = lambda ap: ap.rearrange("p (n e) -> p n e", e=2048)
    pbank = [G.pbig[:, b * 512:(b + 1) * 512] for b in range(4)]
    p6b = G.p6[:].bitcast(BF16)
    p7f = G.p7[:].bitcast(F32)
    ALLB = [0, 1, 2, 3]

    DMA(P, "sp", S.convw[:], D.ssd_convw[j].rearrange("p (c t) -> p c t", t=5), [], [lt + "convw"])
    DMA(P, "sp", S.convb[:], D.ssd_convb[j], [], [lt + "convb"])
    DMA(P, "sp", S.dtb[:], D.ssd_dt_bias[j:j + 1, :].partition_broadcast(128), [], [lt + "dtb"])
    DMA(P, "sp", S.Abc[:], D.ssd_a_log[j:j + 1, :].partition_broadcast(128), [], [lt + "Abc"])
    DMA(P, "sp", S.Dt[:], D.ssd_d[j:j + 1, :].partition_broadcast(128), [], [lt + "Dt"])
    DMA(P, "pool", S.wdt[:].rearrange("p a b -> p (a b)"), D.ssd_wdt[j], [], [lt + "wdt"])
    ACT(P, S.Abc[:], S.Abc[:], AF.Exp, [lt + "Abc"], [lt + "Abc"])
    TS(P, S.Abc[:], S.Abc[:], -1.0, None, ALU.mult, None, [lt + "Abc"], [lt + "Abc"])

    for c in range(NT):
        bkc = 4 + (c % 2)
        pd = (G.p4 if c % 2 == 0 else G.p5)[:, 0:64]
        for kc in range(KC):
            MM(P, pd, G.uT[:, kc, c * 128:(c + 1) * 128], S.wdt[:, kc, :], kc == 0, kc == KC - 1,
               [("uT", c), lt + "wdt"], [], bk=[bkc])
        TT(P, S.tA[:, c, :], pd, S.dtb[:], ALU.add, [lt + "dtb"], [(lt + "tA", c)], bk=[bkc])
    kA, kB, kl, ka = lt + "tA", "sq", lt + "lndt", lt + "a_all"
    allA = [(lt + "tA", c) for c in range(NT)]
    TS(P, S.tB[:], S.tA[:], -1.0, None, ALU.mult, None, allA, [kB])
    TT(P, S.tB[:], S.tB[:], S.tA[:], ALU.min, allA + [kB], [kB])
    ACT(P, S.tB[:], S.tB[:], AF.Exp, [kB], [kB])
    TS(P, S.tB[:], S.tB[:], 1.0, None, ALU.add, None, [kB], [kB])
    ACT(P, S.tB[:], S.tB[:], AF.Ln, [kB], [kB])
    TS(P, S.tA[:], S.tA[:], 0.0, None, ALU.max, None, allA, allA)
    TT(P, S.tA[:], S.tA[:], S.tB[:], ALU.add, allA + [kB], allA)
    ACT(P, S.lndt[:], S.tA[:], AF.Ln, allA, [kl])
    TT(P, S.a_all[:], S.tA[:], S.Abc[:].unsqueeze(1).to_broadcast([128, NT, 64]), ALU.mult,
       allA + [lt + "Abc"], [ka])

    b4 = lambda ap: ap.unsqueeze(2).to_broadcast([128, 4, 64])
    v64 = lambda ap: ap.rearrange("p (h e) -> p h e", h=4)
    v4 = lambda t: t[:, :, :].rearrange("p c (d h) -> p c d h", d=2)

    for g in range(_NG):
        gt = lt
        DMA(P, "pool", c2(S.wA[:]), c2(D.ssd_wA[j, g]), [], [gt + "wA"])
        DMA(P, "pool", S.wz[:].rearrange("p a b -> p (a b)"), D.ssd_wz[j, g], [], [gt + "wz"])
        DMA(P, "pool", S.wo[:].rearrange("p a b -> p (a b)"), D.ssd_wo[j, g], [], [gt + "wo"])
        DMA(P, "sp", S.gnw[:], D.ssd_norm_w[j:j + 1, g * 256:(g + 1) * 256].partition_broadcast(128),
            [], [gt + "gnw"])
        a_g = S.a_all[:, :, :].rearrange("p c (d g h) -> p c d g h", d=2, g=8)[:, :, :, g, :]
        l_g = S.lndt[:, :, :].rearrange("p c (d g h) -> p c d g h", d=2, g=8)[:, :, :, g, :]
        CP(P, v4(S.ahi), a_g, [ka], [gt + "ahi"])
        TT(P, v4(S.alo), a_g, v4(S.ahi), ALU.subtract, [ka, gt + "ahi"], [gt + "alo"])
        pcs = G.p5[:, 0:128].rearrange("p (c k) -> p c k", k=8)
        ptot = G.p5[:, 128:256].rearrange("p (c k) -> p c k", k=8)
        for c in range(NT):
            for (dst, lo, hi, ci) in ((pcs, 0, 4, C_UI), (pcs, 4, 8, C_UIT), (ptot, 0, 8, C_ONE)):
                MM(P, dst[:, c, lo:hi], cst[:, ci, :], S.ahi[:, c, lo:hi], True, False,
                   [gt + "ahi", "cst"], [], bk=[5])
                MM(P, dst[:, c, lo:hi], cst[:, ci, :], S.alo[:, c, lo:hi], False, True,
                   [gt + "alo", "cst"], [], bk=[5])
        TT(P, v4(S.nb), l_g, v4(pcs), ALU.subtract, [kl], [gt + "nb"], bk=[5])
        ACT(P, S.ecs[:], pcs, AF.Exp, [], [gt + "ecs"], bk=[5])
        ACT(P, S.edec[:], ptot, AF.Exp, [], [gt + "edec"], bk=[5])
        TT(P, S.tg[:], ptot, S.nb[:], ALU.add, [gt + "nb"], [gt + "tg"], bk=[5])
        ACT(P, S.dsc[:], S.tg[:], AF.Exp, [gt + "tg"], [gt + "dsc"])

        tiles = [(S.wx[:, :, 0:128], gt + "wA", g * 2, S.xcT, "x", 0),
                 (S.wx[:, :, 128:256], gt + "wA", g * 2 + 1, S.xcT, "x", 1),
                 (S.wB, gt + "wA", 16 + g, S.BT, "B", 0),
                 (S.wC, gt + "wA", 24 + g, S.CT, "C", 0)]
        for (wt, wk, ct, dst, kind, sub) in tiles:
            for tb in range(4):
                for kc in range(KC):
                    MM(P, pbank[tb], wt[:, kc, :], G.uT[:, kc, tb * 512:(tb + 1) * 512], kc == 0, kc == KC - 1,
                       uT_keys(tb) + [wk], [], bk=[tb])
            ACT(P, S.acc[:], G.pbig[:], AF.Identity, [lt + "convw", lt + "convb"], [gt + "acc"],
                bias=S.convb[:, ct:ct + 1], scale=S.convw[:, ct, 2:3], bk=ALLB)
            for tap in (1, 3, 0, 4):
                s = tap - 2
                t0, t1 = max(0, -s), SEQ - max(0, s)
                STT(P, S.acc[:, t0:t1], G.pbig[:, t0 + s:t1 + s], S.convw[:, ct, tap:tap + 1], S.acc[:, t0:t1],
                    ALU.mult, ALU.add, [gt + "acc", lt + "convw"], [gt + "acc"], bk=ALLB)
            dk = gt + "T" + kind
            ACT(P, dst[:], S.acc[:], AF.Silu, [gt + "acc"], [dk])
            if kind == "C":
                continue
            for half in range(2):
                pt = G.p7[:] if half == 0 else p6b
                pbk = [7] if half == 0 else [6]
                for ii in range(8):
                    i = half * 8 + ii
                    TR(P, pt[:, ii * 128:(ii + 1) * 128], dst[:, i * 128:(i + 1) * 128], ident, [dk, "cst"], [], bk=pbk)
                src = pt.rearrange("p (i c) -> p i c", i=8)
                if kind == "x":
                    CP(P, S.xt[:, half * 8:half * 8 + 8, sub * 128:(sub + 1) * 128], src, [],
                       [(gt + "xt", i) for i in range(half * 8, half * 8 + 8)],
                       eng=("act" if half == 0 else "dve"), bk=pbk)
                else:
                    CP(P, S.Bt[:, half * 8:half * 8 + 8, :], src, [],
                       [(gt + "Bt", i) for i in range(half * 8, half * 8 + 8)],
                       eng=("act" if half == 0 else "dve"), bk=pbk)

        Ss = S.acc[:].bitcast(BF16).rearrange("p (c e) -> p c e", c=NT)
        ak = gt + "acc"
        P.op("dve", lambda e: e.memset(S.GS[:], 0.0), [], [gt + "GS"])
        P.op("dve", lambda e: e.memset(S.SS[:], 0.0), [], [gt + "SS"])
        P.op("dve", lambda e: e.memset(S.Gs[:, NT - 1, :], 0.0), [], [(gt + "Gs", NT - 1)])
        P.op("dve", lambda e: e.memset(Ss[:, 0, :], 0.0), [], [ak])

        def st_mm(c, d, i):
            xd = S.xdd[d][i % 2]
            xk = (gt + "xdd", d, i % 2)
            bkn = (4 + (i % 2)) if d == 0 else (6 + (i % 2))
            bank = [G.p4[:], G.p5[:], G.p6[:], p7f][bkn - 4]
            TT(P, v64(xd[:]), v64(S.xt[:, c, :]), b4(S.dsc[:, c, d * 4:d * 4 + 4]), ALU.mult,
               [(gt + "xt", c), gt + "dsc"], [xk], eng=("dve" if d == 0 else "pool"))
            MM(P, bank[:, 0:256], S.Bt[:, c, :], xd[:], True, True, [(gt + "Bt", c), xk], [], bk=[bkn])
            return bank[:, 0:256], bkn

        pend = {}
        pend[(0, 0)] = st_mm(0, 0, 0)
        pend[(1, 0)] = st_mm(NT - 1, 1, 0)
        for i in range(NT - 1):
            cf, cb = i, NT - 1 - i
            if i + 1 < NT - 1:
                pend[(0, i + 1)] = st_mm(cf + 1, 0, i + 1)
                pend[(1, i + 1)] = st_mm(cb - 1, 1, i + 1)
            pst, bkn = pend.pop((0, i))
            TT(P, v64(S.SS[:]), v64(S.SS[:]), b4(S.edec[:, cf, 0:4]), ALU.mult, [gt + "SS", gt + "edec"], [gt + "SS"])
            TT(P, S.SS[:], pst, S.SS[:], ALU.add, [gt + "SS"], [gt + "SS"], bk=[bkn])
            CP(P, Ss[:, cf + 1, :], S.SS[:], [gt + "SS"], [ak], eng="act")
            pst, bkn = pend.pop((1, i))
            TT(P, v64(S.GS[:]), v64(S.GS[:]), b4(S.edec[:, cb, 4:8]), ALU.mult, [gt + "GS", gt + "edec"], [gt + "GS"])
            TT(P, S.GS[:], pst, S.GS[:], ALU.add, [gt + "GS"], [gt + "GS"], bk=[bkn])
            CP(P, S.Gs[:, cb - 1, :], S.GS[:], [gt + "GS"], [(gt + "Gs", cb - 1)], eng="act")

        def tail(c):
            par = c % 2
            yn, ynT = S.yn[par], S.ynT[par]
            ptr = G.p7[:, 0:256]
            for jj in range(2):
                TR(P, ptr[:, jj * 128:(jj + 1) * 128], yn[:, jj * 128:(jj + 1) * 128], ident,
                   [(gt + "yn", par), "cst"], [], bk=[7])
            CP(P, ynT[:], ptr.rearrange("p (j t) -> p j t", j=2), [], [(gt + "ynT", par)], eng="act", bk=[7])
            for ch in range(2):
                po = G.p6[:] if ch == 0 else p7f
                pok = [6] if ch == 0 else [7]
                for jj in range(2):
                    MM(P, po, ynT[:, jj, :], S.wo[:, jj, ch * 512:(ch + 1) * 512], jj == 0, jj == 1,
                       [(gt + "ynT", par), gt + "wo"], [], bk=pok)
                hs = G.h[:, c, ch * 512:(ch + 1) * 512]
                TT(P, hs, po, hs, ALU.add, [("h", c)], [("h", c)], bk=pok)

        for c in range(NT + 3):
          if c < NT:
            par = c % 2
            tok = slice(c * 128, (c + 1) * 128)
            pcb = pbank[2][:, 0:128]
            MM(P, pcb, S.BT[:, tok], S.CT[:, tok], True, True, [gt + "TB", gt + "TC"], [], bk=[2])
            pyo = G.p4
            MM(P, pyo[:, 0:256], S.CT[:, tok], Ss[:, c, :], True, True, [gt + "TC", ak], [], bk=[4])
            MM(P, pyo[:, 256:512], S.CT[:, tok], S.Gs[:, c, :], True, True, [gt + "TC", (gt + "Gs", c)], [], bk=[4])
            pz = G.p5[:, 0:256]
            for kc in range(KC):
                MM(P, pz, G.uT[:, kc, tok], S.wz[:, kc, :], kc == 0, kc == KC - 1, [("uT", c), gt + "wz"], [], bk=[5])
            sz = S.sz[par]
            ACT(P, sz[:], pz, AF.Tanh, [], [(gt + "sz", par)], scale=0.5, bk=[5])
            STT(P, sz[:], sz[:], 1.0, pz, ALU.add, ALU.mult, [(gt + "sz", par)], [(gt + "sz", par)], bk=[5])
            py = pbank[3][:, 0:256]
            its = [(hh, d) for hh in range(4) for d in range(2)]

            def lps(n):
                hh, d = its[n]
                col = d * 4 + hh
                lb = n % 2
                pl = pbank[lb][:, 0:128]
                ucst = cst[:, C_UI if d == 0 else C_UIT, :]
                mcst = cst[:, C_NMF if d == 0 else C_NMB, :]
                MM(P, pl, S.ahi[:, c, col:col + 1].to_broadcast([128, 128]), ucst, True, False,
                   [gt + "ahi", "cst"], [], bk=[lb])
                MM(P, pl, S.alo[:, c, col:col + 1].to_broadcast([128, 128]), ucst, False, False,
                   [gt + "alo", "cst"], [], bk=[lb])
                MM(P, pl, ident, mcst, False, True, ["cst"], [], bk=[lb])

            lps(0)
            lps(1)
            for n in range(8):
                hh, d = its[n]
                col = d * 4 + hh
                sl = n % 4
                lb = n % 2
                pl = pbank[lb][:, 0:128]
                Lx = S.Lexp[sl]
                ACT(P, Lx[:], pl, AF.Exp, [gt + "nb"], [(gt + "Lexp", sl)], bias=S.nb[:, c, col:col + 1], bk=[lb])
                MTt = S.MT[sl]
                TT(P, MTt[:], pcb, Lx[:], ALU.mult, [(gt + "Lexp", sl)], [(gt + "MT", sl)], bk=[2])
                if n + 2 < 8:
                    lps(n + 2)
                MM(P, py[:, hh * 64:(hh + 1) * 64], MTt[:], S.xt[:, c, hh * 64:(hh + 1) * 64], d == 0, d == 1,
                   [(gt + "MT", sl), (gt + "xt", c)], [], bk=[3])
            t1, t2, t3, y = S.t1[par], S.t2[par], S.t3[par], S.y[c % 4]
            k1, k2, k3, ky = (gt + "t1", par), (gt + "t2", par), (gt + "t3", par), (gt + "y", c % 4)
            TT(P, v64(t1[:]), v64(pyo[:, 0:256]), b4(S.ecs[:, c, 0:4]), ALU.mult, [gt + "ecs"], [k1], bk=[4])
            TT(P, v64(t2[:]), v64(pyo[:, 256:512]), b4(S.ecs[:, c, 4:8]), ALU.mult, [gt + "ecs"], [k2], bk=[4])
            TT(P, y[:], py, t1[:], ALU.add, [k1], [ky], bk=[3])
            TT(P, v64(t3[:]), v64(S.xt[:, c, :]), b4(S.Dt[:, g * 4:g * 4 + 4]), ALU.mult,
               [(gt + "xt", c), lt + "Dt"], [k3], eng="pool")
            TT(P, t2[:], t2[:], t3[:], ALU.add, [k2, k3], [k2], eng="pool")
            TT(P, y[:], y[:], t2[:], ALU.add, [ky, k2], [ky], eng="pool")
            TT(P, y[:], y[:], sz[:], ALU.mult, [ky, (gt + "sz", par)], [ky], eng="pool")
            TT(P, S.sqy[par][:], y[:], y[:], ALU.mult, [ky], [(gt + "sqy", par)], eng="pool")
          if 1 <= c <= NT:
            cc = c - 1
            ssq = S.ssq[cc % 4]
            sk = (gt + "ssq", cc % 4)
            P.op("dve", lambda e, ssq=ssq, q=S.sqy[cc % 2]: e.tensor_reduce(out=ssq[:], in_=q[:], axis=AX.X, op=ALU.add),
                 [(gt + "sqy", cc % 2)], [sk])
            TS(P, ssq[:], ssq[:], 1.0 / 256.0, 4.0 * EPS, ALU.mult, ALU.add, [sk], [sk])
            ACT(P, ssq[:], ssq[:], AF.Sqrt, [sk], [sk])
          if 2 <= c <= NT + 1:
            cc = c - 2
            ssq = S.ssq[cc % 4]
            sk = (gt + "ssq", cc % 4)
            P.op("dve", lambda e, ssq=ssq: e.reciprocal(out=ssq[:], in_=ssq[:]), [sk], [sk])
            yn = S.yn[cc % 2]
            STT(P, yn[:], S.y[cc % 4][:], ssq[:, 0:1], S.gnw[:], ALU.mult, ALU.mult,
                [(gt + "y", cc % 4), sk, gt + "gnw"], [(gt + "yn", cc % 2)])
          if c >= 3:
            tail(c - 3)


_CACHE = {}


def _host_inputs(inp):
    f32 = np.float32
    bands_np, _ = _pool_bands()
    convw = np.asarray(inp["ssd_conv_w"], f32)
    convw_l = np.ascontiguousarray(convw.reshape(2, 5, 32, 128).transpose(0, 3, 2, 1)).reshape(2, 128, 160)
    convb_l = np.ascontiguousarray(np.asarray(inp["ssd_conv_b"], f32).reshape(2, 32, 128).transpose(0, 2, 1))
    pscale_l = np.ascontiguousarray(np.asarray(inp["pool_scale"], f32).reshape(2, 16, 128).transpose(0, 2, 1))
    W = np.asarray(inp["ssd_w_in"], f32).reshape(2, KC, 128, 6208)
    wxp = W[..., 2048:4096].reshape(2, KC, 128, 8, 256).transpose(0, 3, 2, 1, 4).reshape(2, 8, 128, 2048)
    wBp = W[..., 4096:5120].reshape(2, KC, 128, 8, 128).transpose(0, 3, 2, 1, 4).reshape(2, 8, 128, 1024)
    wCp = W[..., 5120:6144].reshape(2, KC, 128, 8, 128).transpose(0, 3, 2, 1, 4).reshape(2, 8, 128, 1024)
    wA = np.ascontiguousarray(np.concatenate([wxp, wBp, wCp], axis=-1))
    wz = np.ascontiguousarray(W[..., 0:2048].reshape(2, KC, 128, 8, 256).transpose(0, 3, 2, 1, 4).reshape(2, 8, 128, 2048))
    wdt = np.ascontiguousarray(W[..., 6144:6208].transpose(0, 2, 1, 3).reshape(2, 128, 512))
    wso = np.ascontiguousarray(np.asarray(inp["ssd_w_out"], f32).reshape(2, 8, 2, 128, 1024)
                               .transpose(0, 1, 3, 2, 4).reshape(2, 8, 128, 2048))
    PW = np.asarray(inp["pool_w_in"], f32).reshape(2, KC, 128, 2, 4, 512)
    pwvg = np.ascontiguousarray(PW.transpose(0, 4, 3, 2, 1, 5).reshape(2, 4, 2, 128, 4096))
    pmix = np.ascontiguousarray(np.asarray(inp["pool_mix_w"], f32).reshape(2, 4, 4, 128, 512)
                                .transpose(0, 1, 3, 2, 4).reshape(2, 4, 128, 2048))
    pwo = np.ascontiguousarray(np.asarray(inp["pool_w_out"], f32).reshape(2, 4, 4, 128, 1024)
                               .transpose(0, 1, 3, 2, 4).reshape(2, 4, 128, 4096))
    shared = {
        "norm_w": np.ascontiguousarray(inp["norm_w"], f32),
        "final_norm_w": np.ascontiguousarray(inp["final_norm_w"], f32).reshape(1, D_MODEL),
        "ssd_wA": wA, "ssd_wz": wz, "ssd_wo": wso, "ssd_wdt": wdt,
        "ssd_convw": convw_l,
        "ssd_convb": convb_l,
        "ssd_dt_bias": np.ascontiguousarray(inp["ssd_dt_bias"], f32).reshape(2, 64),
        "ssd_a_log": np.ascontiguousarray(inp["ssd_a_log"], f32).reshape(2, 64),
        "ssd_d": np.ascontiguousarray(inp["ssd_d"], f32),
        "ssd_norm_w": np.ascontiguousarray(inp["ssd_norm_w"], f32),
        "pool_wvg": pwvg, "pool_mix": pmix, "pool_wo": pwo,
        "pool_scale": pscale_l,
        "cst": _ssd_consts().reshape(128, 6 * 128),
        "bands": bands_np.reshape(128, -1),
    }
    return shared


def run_layers(inp, x, layers, final_norm):
    key = (tuple(layers), final_norm)
    if key not in _CACHE:
        _CACHE[key] = build_program(layers, final_norm)[0]
    nc = _CACHE[key]
    shared = _host_inputs(inp)
    B = x.shape[0]
    in_maps = []
    for b in range(B):
        m = dict(shared)
        m["x"] = np.ascontiguousarray(x[b], np.float32)
        in_maps.append(m)
    res = run_bass_kernel_spmd(nc, in_maps, core_ids=list(range(B)))
    return np.stack([np.asarray(r["out"], np.float32) for r in res.results], axis=0)


def kernel(**inputs):
    x = np.asarray(inputs["x"], np.float32)
    return run_layers(inputs, x, (0, 1, 2, 3), True)
```

```python
from contextlib import ExitStack
import numpy as np
import ml_dtypes
import concourse.bass as bass
import concourse.mybir as mybir
from concourse.bass_utils import run_bass_kernel_spmd

F32 = mybir.dt.float32
BF16 = mybir.dt.bfloat16
AF = mybir.ActivationFunctionType
ALU = mybir.AluOpType
AX = mybir.AxisListType

D_MODEL = 1024
SEQ = 2048
NT = SEQ // 128
KC = D_MODEL // 128
D_INNER = 2048
EPS = 1e-6
POOL_WINDOWS = (2, 4, 8, 16)
NEG = -30000.0

ENGS = ("pe", "act", "dve", "pool", "sp")


class Prog:
    def __init__(self, nc, es):
        self.nc = nc
        self.es = es
        self.ops = {e: [] for e in ENGS}
        self.cnt = {e: 0 for e in ENGS}
        self.sem = {e: es.enter_context(nc.semaphore("s_" + e)) for e in ENGS}
        self.waited = {e: {} for e in ENGS}
        self.lastw = {}
        self.readers = {}
        self.dsem = {}
        self.nwaits = 0
        self.uid = 0
        self.bank_last = [dict() for _ in range(8)]

    def sb(self, name, shape, dt, es=None):
        self.uid += 1
        return (es or self.es).enter_context(
            self.nc.sbuf_tensor("%s_%d" % (name, self.uid), list(shape), dt))

    def ps(self, name, shape, dt=F32):
        return self.es.enter_context(self.nc.psum_tensor(name, list(shape), dt))

    def _deps(self, e, reads, writes, banks=()):
        deps = set()
        for b in banks:
            for e2, t in self.bank_last[b].items():
                if e2 != e:
                    deps.add(t)
        for k in reads:
            t = self.lastw.get(k)
            if t is not None:
                deps.add(t)
        for k in writes:
            t = self.lastw.get(k)
            if t is not None:
                deps.add(t)
            for t in self.readers.get(k, ()):
                deps.add(t)
        waits = []
        w = self.waited[e]
        for (sk, v) in deps:
            if sk == e and e == "pe":
                continue
            if w.get(sk, 0) < v:
                w[sk] = v
                waits.append((sk, v))
        self.nwaits += len(waits)
        return waits

    def _commit(self, tok, reads, writes):
        for k in reads:
            self.readers.setdefault(k, []).append(tok)
        for k in writes:
            self.lastw[k] = tok
            self.readers[k] = []

    def op(self, e, fn, reads=(), writes=(), banks=()):
        waits = self._deps(e, reads, writes, banks)
        self.cnt[e] += 1
        tok = (e, self.cnt[e])
        self.ops[e].append((waits, fn, (e, 1)))
        self._commit(tok, reads, writes)
        for b in banks:
            self.bank_last[b][e] = tok
        return tok

    def dma(self, q, fn, reads=(), writes=(), skey=None):
        if skey is None:
            skey = writes[0] if writes else reads[0]
        waits = self._deps(q, reads, writes)
        if skey not in self.dsem:
            self.dsem[skey] = [self.es.enter_context(self.nc.semaphore("d_%d" % len(self.dsem))), 0]
        ent = self.dsem[skey]
        ent[1] += 16
        sk = ("d", skey)
        tok = (sk, ent[1])
        self.ops[q].append((waits, fn, (sk, 16)))
        self._commit(tok, reads, writes)
        return tok

    def _semh(self, sk):
        if isinstance(sk, tuple):
            return self.dsem[sk[1]][0]
        return self.sem[sk]

    def barrier(self):
        toks = [(e, self.cnt[e]) for e in ENGS if self.cnt[e] > 0]
        toks += [(("d", k), v[1]) for k, v in self.dsem.items()]
        for e in ENGS:
            waits = []
            for (sk, v) in toks:
                if sk == e:
                    continue
                if self.waited[e].get(sk, 0) < v:
                    self.waited[e][sk] = v
                    waits.append((sk, v))
            if waits:
                self.ops[e].append((waits, None, None))

    def emit(self):
        nc = self.nc
        with nc.Block() as block:
            def run(eng, name):
                for (waits, fn, inc) in self.ops[name]:
                    for (sk, v) in waits:
                        eng.wait_ge(self._semh(sk), v)
                    if fn is None:
                        continue
                    ins = fn(eng)
                    ins.then_inc(self._semh(inc[0]), inc[1])

            @block.tensor
            def _(eng):
                run(eng, "pe")

            @block.scalar
            def _(eng):
                run(eng, "act")

            @block.vector
            def _(eng):
                run(eng, "dve")

            @block.gpsimd
            def _(eng):
                run(eng, "pool")

            @block.sync
            def _(eng):
                run(eng, "sp")


def MM(P, out, lhsT, rhs, start, stop, reads, writes, bk=()):
    P.op("pe", lambda e: e.matmul(out, lhsT=lhsT, rhs=rhs, start=start, stop=stop), reads, writes, bk)


def TR(P, out, in_, ident, reads, writes, bk=()):
    P.op("pe", lambda e: e.transpose(out=out, in_=in_, identity=ident), reads, writes, bk)


def ACT(P, out, in_, func, reads, writes, bias=None, scale=1.0, bk=()):
    if bias is None:
        P.op("act", lambda e: e.activation(out=out, in_=in_, func=func, scale=scale), reads, writes, bk)
    else:
        P.op("act", lambda e: e.activation(out=out, in_=in_, func=func, bias=bias, scale=scale),
             reads, writes, bk)


def TT(P, out, in0, in1, op, reads, writes, eng="dve", bk=()):
    P.op(eng, lambda e: e.tensor_tensor(out=out, in0=in0, in1=in1, op=op), reads, writes, bk)


def TS(P, out, in0, s1, s2, op0, op1, reads, writes, eng="dve"):
    if s2 is None:
        P.op(eng, lambda e: e.tensor_scalar(out=out, in0=in0, scalar1=s1, scalar2=None, op0=op0),
             reads, writes)
    else:
        P.op(eng, lambda e: e.tensor_scalar(out=out, in0=in0, scalar1=s1, scalar2=s2, op0=op0, op1=op1),
             reads, writes)


def STT(P, out, in0, scalar, in1, op0, op1, reads, writes, bk=()):
    P.op("dve", lambda e: e.scalar_tensor_tensor(out=out, in0=in0, scalar=scalar, in1=in1,
                                                  op0=op0, op1=op1), reads, writes, bk)


def CP(P, out, in_, reads, writes, eng="dve", bk=()):
    if eng == "act":
        P.op("act", lambda e: e.copy(out=out, in_=in_), reads, writes, bk)
    else:
        P.op(eng, lambda e: e.tensor_copy(out=out, in_=in_), reads, writes, bk)


def DMA(P, q, out, in_, reads, writes, skey=None):
    return P.dma(q, lambda e: e.dma_start(out=out, in_=in_), reads, writes, skey)


def _bf16_split(a):
    hi = a.astype(ml_dtypes.bfloat16)
    lo = (a - hi.astype(np.float32)).astype(ml_dtypes.bfloat16)
    return hi, lo


def _pool_bands():
    mats = []
    idx = {}
    L = SEQ
    for gi, w in enumerate(POOL_WINDOWS):
        t = np.arange(L)
        lo = np.clip(t - w // 2, 0, L)
        hi = np.clip(t + w - w // 2, 0, L)
        A = np.zeros((L, L), np.float64)
        for tt in range(L):
            A[lo[tt]:hi[tt], tt] = 1.0 / (hi[tt] - lo[tt])
        A -= np.eye(L)
        A = A.astype(np.float32)
        blocks = {
            "sub": A[0:128, 128:256], "main": A[128:256, 128:256], "sup": A[256:384, 128:256],
            "first": A[0:128, 0:128], "last": A[L - 128:L, L - 128:L],
        }
        for name, blk in blocks.items():
            ids = []
            h_, l_ = _bf16_split(np.ascontiguousarray(blk))
            for m in (h_, l_):
                if np.any(m.astype(np.float32) != 0):
                    ids.append(len(mats))
                    mats.append(m)
            idx[(gi, name)] = ids
    arr = np.stack(mats, axis=1)
    return np.ascontiguousarray(arr), idx


def _ssd_consts():
    k = np.arange(128)
    uincl = (k[:, None] <= k[None, :]).astype(np.float32)
    uinclT = uincl.T.copy()
    ident = np.eye(128, dtype=np.float32)
    nmf = NEG * (k[None, :] < k[:, None]).astype(np.float32)
    nmb = NEG * (k[None, :] > k[:, None]).astype(np.float32)
    ones = np.ones((128, 128), np.float32)
    arr = np.stack([ident, uincl, uinclT, nmf, nmb, ones], axis=1).astype(ml_dtypes.bfloat16)
    return np.ascontiguousarray(arr)


C_ID, C_UI, C_UIT, C_NMF, C_NMB, C_ONE = range(6)


class Ctx:
    pass


def build_program(layers=(0, 1, 2, 3), final_norm=True):
    nc = bass.Bass("TRN2", target_bir_lowering=False)
    bands_np, band_idx = _pool_bands()
    NB = bands_np.shape[1]

    def din(name, shape, dt=F32):
        return nc.dram_tensor(name, list(shape), dt, kind="ExternalInput").ap()

    D = Ctx()
    D.x = din("x", [SEQ, D_MODEL])
    D.norm_w = din("norm_w", [4, D_MODEL])
    D.final_w = din("final_norm_w", [1, D_MODEL])
    D.ssd_wA = din("ssd_wA", [2, 8, 128, 4096])
    D.ssd_wz = din("ssd_wz", [2, 8, 128, 2048])
    D.ssd_wo = din("ssd_wo", [2, 8, 128, 2048])
    D.ssd_wdt = din("ssd_wdt", [2, 128, 512])
    D.ssd_convw = din("ssd_convw", [2, 128, 32 * 5])
    D.ssd_convb = din("ssd_convb", [2, 128, 32])
    D.ssd_dt_bias = din("ssd_dt_bias", [2, 64])
    D.ssd_a_log = din("ssd_a_log", [2, 64])
    D.ssd_d = din("ssd_d", [2, 32])
    D.ssd_norm_w = din("ssd_norm_w", [2, D_INNER])
    D.pool_wvg = din("pool_wvg", [2, 4, 2, 128, 4096])
    D.pool_mix = din("pool_mix", [2, 4, 128, 2048])
    D.pool_scale = din("pool_scale", [2, 128, 16])
    D.pool_wo = din("pool_wo", [2, 4, 128, 4096])
    D.cst = din("cst", [128, 6 * 128], BF16)
    D.bands = din("bands", [128, NB * 128], BF16)
    D.out = nc.dram_tensor("out", [SEQ, D_MODEL], F32, kind="ExternalOutput").ap()

    with ExitStack() as es:
        P = Prog(nc, es)
        G = Ctx()
        G.h = P.sb("h", [128, NT, D_MODEL], F32)
        G.uT = P.sb("uT", [128, KC, SEQ], BF16)
        G.cst = P.sb("cst", [128, 6, 128], BF16)
        G.nwb = P.sb("nwb", [128, D_MODEL], F32)
        G.sq = P.sb("sq", [128, D_MODEL], F32)
        G.ss = P.sb("ss", [128, NT], F32)
        G.rstd = P.sb("rstd", [128, NT], F32)
        G.ubf = [P.sb("ubf%d" % i, [128, D_MODEL], BF16) for i in range(2)]
        G.pbig = P.ps("pbig", [128, 2048], F32)
        G.p4 = P.ps("p4", [128, 512], F32)
        G.p5 = P.ps("p5", [128, 512], F32)
        G.p6 = P.ps("p6", [128, 512], F32)
        G.p7 = P.ps("p7", [128, 1024], BF16)
        G.ident = G.cst[:, C_ID, :]

        DMA(P, "sp", G.cst[:], D.cst.rearrange("p (a b) -> p a b", a=6), [], ["cst"])
        xv = D.x.rearrange("(i p) d -> p i d", p=128)
        for i in range(0, NT, 4):
            DMA(P, "sp", G.h[:, i:i + 4, :], xv[:, i:i + 4, :], [], [("h", j) for j in range(i, i + 4)],
                skey=("hld", i))

        first = True
        for L in layers:
            if not first:
                P.barrier()
            first = False
            with ExitStack() as les:
                emit_norm(P, G, D.norm_w[L:L + 1, :], tag="n%d" % L)
                if L % 2 == 0:
                    emit_ssd(P, G, D, L // 2, les)
                else:
                    emit_pool(P, G, D, L // 2, les, band_idx, NB)
                P.barrier()
        toks = []
        if final_norm:
            toks = emit_final(P, G, D)
        else:
            ov = D.out.rearrange("(i p) d -> p i d", p=128)
            for i in range(0, NT, 4):
                toks.append(DMA(P, "sp", ov[:, i:i + 4, :], G.h[:, i:i + 4, :],
                                [("h", j) for j in range(i, i + 4)], [], skey=("ost", i)))
        P.ops["sp"].append(([(sk, v) for (sk, v) in toks], None, None))
        P.emit()
        stats = dict(cnt=dict(P.cnt), waits=P.nwaits)
    return nc, stats


def emit_stats(P, G, tag):
    for i in range(NT):
        ACT(P, G.sq[:], G.h[:, i, :], AF.Square, [("h", i)], ["sq"])
        P.op("dve", lambda e, i=i: e.tensor_reduce(out=G.ss[:, i:i + 1], in_=G.sq[:], axis=AX.X, op=ALU.add),
             ["sq"], [("ss", i)])
    allss = [("ss", i) for i in range(NT)]
    TS(P, G.ss[:], G.ss[:], 1.0 / D_MODEL, EPS, ALU.mult, ALU.add, allss, allss)
    ACT(P, G.ss[:], G.ss[:], AF.Sqrt, allss, allss)
    P.op("dve", lambda e: e.reciprocal(out=G.rstd[:], in_=G.ss[:]), allss, ["rstd"])


def emit_norm(P, G, nw_row, tag):
    DMA(P, "sp", G.nwb[:], nw_row.partition_broadcast(128), [], ["nwb"])
    emit_stats(P, G, tag)
    for i in range(NT):
        ub = G.ubf[i % 2]
        uk = ("ubf", i % 2)
        STT(P, ub[:], G.h[:, i, :], G.rstd[:, i:i + 1], G.nwb[:], ALU.mult, ALU.mult,
            [("h", i), "rstd", "nwb"], [uk])
        pt, pb_ = (G.p7[:], 7) if i % 2 == 0 else (G.p6[:].bitcast(BF16), 6)
        for kc in range(KC):
            TR(P, pt[:, kc * 128:(kc + 1) * 128], ub[:, kc * 128:(kc + 1) * 128], G.ident,
               [uk, "cst"], [], bk=[pb_])
        CP(P, G.uT[:, :, i * 128:(i + 1) * 128], pt.rearrange("p (k t) -> p k t", k=KC),
           [], [("uT", i)], eng=("act" if i % 2 == 0 else "dve"), bk=[pb_])


def emit_final(P, G, D):
    DMA(P, "sp", G.nwb[:], D.final_w.partition_broadcast(128), [], ["nwb"])
    emit_stats(P, G, "fin")
    toks = []
    ov = D.out.rearrange("(i p) d -> p i d", p=128)
    for i in range(NT):
        STT(P, G.h[:, i, :], G.h[:, i, :], G.rstd[:, i:i + 1], G.nwb[:], ALU.mult, ALU.mult,
            [("h", i), "rstd", "nwb"], [("h", i)])
        if i % 4 == 3:
            toks.append(DMA(P, "sp", ov[:, i - 3:i + 1, :], G.h[:, i - 3:i + 1, :],
                            [("h", j) for j in range(i - 3, i + 1)], [], skey=("ost", i)))
    return toks


def uT_keys(tb):
    return [("uT", i) for i in range(tb * 4, tb * 4 + 4)]


def emit_pool(P, G, D, j, les, band_idx, NB):
    S = Ctx()
    S.bands = P.sb("bands", [128, NB, 128], BF16, les)
    S.scale = P.sb("pscale", [128, 16], F32, les)
    S.wv = P.sb("wv", [128, KC, 512], BF16, les)
    S.wg = P.sb("wg", [128, KC, 512], BF16, les)
    S.mx = P.sb("mx", [128, 4, 512], BF16, les)
    S.wo = P.sb("wo", [128, 4, D_MODEL], BF16, les)
    S.v = P.sb("v", [128, NT, 512], BF16, les)
    S.pT = P.sb("pT", [128, 4, 512], BF16, les)
    S.sg = [P.sb("sg%d" % i, [128, 512], F32, les) for i in range(2)]
    S.yT = P.sb("yT", [128, 4, 512], BF16, les)
    lt = "pl%d" % j
    DMA(P, "sp", S.bands[:], D.bands.rearrange("p (a b) -> p a b", a=NB), [], [lt + "bands"])
    DMA(P, "sp", S.scale[:], D.pool_scale[j], [], [lt + "scale"])
    pbanks = [G.pbig[:, b * 512:(b + 1) * 512] for b in range(4)]
    c2 = lambda ap: ap.rearrange("p (n e) -> p n e", e=2048)
    f2 = lambda t: t[:].rearrange("p a b -> p (a b)").rearrange("p (n e) -> p n e", e=2048)
    for g in range(4):
        DMA(P, "pool", f2(S.wv), c2(D.pool_wvg[j, g, 0]), [], [lt + "wv"])
        DMA(P, "pool", f2(S.wg), c2(D.pool_wvg[j, g, 1]), [], [lt + "wg"])
        DMA(P, "pool", f2(S.mx), c2(D.pool_mix[j, g]), [], [lt + "mx"])
        DMA(P, "pool", f2(S.wo), c2(D.pool_wo[j, g]), [], [lt + "wo"])
        for i in range(NT):
            pb = pbanks[i % 4]
            pk = ("pbig", i % 4)
            for kc in range(KC):
                MM(P, pb, G.uT[:, kc, i * 128:(i + 1) * 128], S.wv[:, kc, :], kc == 0, kc == KC - 1,
                   [("uT", i), lt + "wv"], [], bk=[i % 4])
            CP(P, S.v[:, i, :], pb, [], [(lt + "v", i)], eng=("act" if i % 2 == 0 else "dve"), bk=[i % 4])
        for tb in range(4):
            for cb in range(4):
                pb = pbanks[cb]
                pk = ("pbig", cb)
                for jj in range(4):
                    jt = tb * 4 + jj
                    terms = []
                    if jt > 0:
                        terms += [(jt - 1, m) for m in band_idx[(g, "sub")]]
                    nm = "first" if jt == 0 else ("last" if jt == NT - 1 else "main")
                    terms += [(jt, m) for m in band_idx[(g, nm)]]
                    if jt < NT - 1:
                        terms += [(jt + 1, m) for m in band_idx[(g, "sup")]]
                    for n, (it, m) in enumerate(terms):
                        MM(P, pb[:, jj * 128:(jj + 1) * 128], S.v[:, it, cb * 128:(cb + 1) * 128],
                           S.bands[:, m, :], n == 0, n == len(terms) - 1,
                           [(lt + "v", it), lt + "bands"], [], bk=[cb])
                CP(P, S.pT[:, cb, :], pb, [], [(lt + "pT", cb)], eng=("act" if cb % 2 == 0 else "dve"), bk=[cb])
            for db in range(4):
                sg = S.sg[db % 2]
                sgk = (lt + "sg", db % 2)
                MMk = ("p4",)
                for kc in range(KC):
                    MM(P, G.p4[:], S.wg[:, kc, db * 128:(db + 1) * 128], G.uT[:, kc, tb * 512:(tb + 1) * 512],
                       kc == 0, kc == KC - 1, uT_keys(tb) + [lt + "wg"], [], bk=[4])
                ACT(P, sg[:], G.p4[:], AF.Silu, [], [sgk], bk=[4])
                pm = G.p5 if db % 2 == 0 else G.p6
                pmk = 5 if db % 2 == 0 else 6
                for cb in range(4):
                    MM(P, pm[:], S.mx[:, cb, db * 128:(db + 1) * 128], S.pT[:, cb, :], cb == 0, cb == 3,
                       [(lt + "pT", cb), lt + "mx"], [], bk=[pmk])
                sc = S.scale[:, g * 4 + db:g * 4 + db + 1]
                STT(P, S.yT[:, db, :], pm[:], sc, sg[:], ALU.mult, ALU.mult,
                    [sgk, lt + "scale"], [(lt + "yT", db)], bk=[pmk])
            for jj in range(4):
                it = tb * 4 + jj
                for ch in range(2):
                    po = G.p5 if ch == 0 else G.p6
                    pok = 5 if ch == 0 else 6
                    for db in range(4):
                        MM(P, po[:], S.yT[:, db, jj * 128:(jj + 1) * 128], S.wo[:, db, ch * 512:(ch + 1) * 512],
                           db == 0, db == 3, [(lt + "yT", db), lt + "wo"], [], bk=[pok])
                    hs = G.h[:, it, ch * 512:(ch + 1) * 512]
                    TT(P, hs, po[:], hs, ALU.add, [("h", it)], [("h", it)], bk=[pok])


import os as _os
_NG = int(_os.environ.get("SSD_NG", "8"))


def emit_ssd(P, G, D, j, les):
    lt = "sd%d" % j
    S = Ctx()
    S.convw = P.sb("convw", [128, 32, 5], F32, les)
    S.convb = P.sb("convb", [128, 32], F32, les)
    S.dtb = P.sb("dtb", [128, 64], F32, les)
    S.Abc = P.sb("Abc", [128, 64], F32, les)
    S.Dt = P.sb("Dt", [128, 32], F32, les)
    S.wdt = P.sb("wdt", [128, KC, 64], BF16, les)
    S.a_all = P.sb("a_all", [128, NT, 64], F32, les)
    S.lndt = P.sb("lndt", [128, NT, 64], F32, les)
    S.tA = P.sb("tA", [128, NT, 64], F32, les)
    S.tB = G.sq[:].rearrange("p (c k) -> p c k", k=64)
    S.wz = P.sb("wz", [128, KC, 256], BF16, les)
    S.wA = P.sb("wA", [128, 4096], BF16, les)
    S.wx = S.wA[:, 0:2048].rearrange("p (k c) -> p k c", k=KC)
    S.wB = S.wA[:, 2048:3072].rearrange("p (k c) -> p k c", k=KC)
    S.wC = S.wA[:, 3072:4096].rearrange("p (k c) -> p k c", k=KC)
    S.wo = P.sb("wo", [128, 2, D_MODEL], BF16, les)
    S.gnw = P.sb("gnw", [128, 256], F32, les)
    S.acc = P.sb("acc", [128, SEQ], F32, les)
    S.xcT = P.sb("xcT", [128, SEQ], BF16, les)
    S.BT = P.sb("BT", [128, SEQ], BF16, les)
    S.CT = P.sb("CT", [128, SEQ], BF16, les)
    S.xt = P.sb("xt", [128, NT, 256], BF16, les)
    S.Bt = P.sb("Bt", [128, NT, 128], BF16, les)
    S.Gs = P.sb("Gs", [128, NT, 256], BF16, les)
    S.ahi = P.sb("ahi", [128, NT, 8], BF16, les)
    S.alo = P.sb("alo", [128, NT, 8], BF16, les)
    S.nb = P.sb("nb", [128, NT, 8], F32, les)
    S.ecs = P.sb("ecs", [128, NT, 8], F32, les)
    S.dsc = P.sb("dsc", [128, NT, 8], F32, les)
    S.edec = P.sb("edec", [128, NT, 8], F32, les)
    S.tg = P.sb("tg", [128, NT, 8], F32, les)
    S.GS = P.sb("GS", [128, 256], F32, les)
    S.SS = P.sb("SS", [128, 256], F32, les)
    S.Lexp = [P.sb("Lexp%d" % i, [128, 128], F32, les) for i in range(4)]
    S.MT = [P.sb("MT%d" % i, [128, 128], BF16, les) for i in range(4)]
    S.xdd = [[P.sb("xdd%d_%d" % (d, i), [128, 256], BF16, les) for i in range(2)] for d in range(2)]
    S.sz = [P.sb("sz%d" % i, [128, 256], F32, les) for i in range(2)]
    S.t1 = [P.sb("t1_%d" % i, [128, 256], F32, les) for i in range(2)]
    S.t2 = [P.sb("t2_%d" % i, [128, 256], F32, les) for i in range(2)]
    S.t3 = [P.sb("t3_%d" % i, [128, 256], F32, les) for i in range(2)]
    S.y = [P.sb("y_%d" % i, [128, 256], F32, les) for i in range(4)]
    S.sqy = [P.sb("sqy%d" % i, [128, 256], F32, les) for i in range(2)]
    S.ssq = [P.sb("ssq%d" % i, [128, 2], F32, les) for i in range(4)]
    S.yn = [P.sb("yn%d" % i, [128, 256], BF16, les) for i in range(2)]
    S.ynT = [P.sb("ynT%d" % i, [128, 2, 128], BF16, les) for i in range(2)]

    cst = G.cst
    ident = G.ident
    c2 = lambda ap: ap.rearrange("p (n e) -> p n e", e=2048)
    pbank = [G.pbig[:, b * 512:(b + 1) * 512] for b in range(4)]
    p6b = G.p6[:].bitcast(BF16)
    p7f = G.p7[:].bitcast(F32)
    ALLB = [0, 1, 2, 3]

    DMA(P, "sp", S.convw[:], D.ssd_convw[j].rearrange("p (c t) -> p c t", t=5), [], [lt + "convw"])
    DMA(P, "sp", S.convb[:], D.ssd_convb[j], [], [lt + "convb"])
    DMA(P, "sp", S.dtb[:], D.ssd_dt_bias[j:j + 1, :].partition_broadcast(128), [], [lt + "dtb"])
    DMA(P, "sp", S.Abc[:], D.ssd_a_log[j:j + 1, :].partition_broadcast(128), [], [lt + "Abc"])
    DMA(P, "sp", S.Dt[:], D.ssd_d[j:j + 1, :].partition_broadcast(128), [], [lt + "Dt"])
    DMA(P, "pool", S.wdt[:].rearrange("p a b -> p (a b)"), D.ssd_wdt[j], [], [lt + "wdt"])
    ACT(P, S.Abc[:], S.Abc[:], AF.Exp, [lt + "Abc"], [lt + "Abc"])
    TS(P, S.Abc[:], S.Abc[:], -1.0, None, ALU.mult, None, [lt + "Abc"], [lt + "Abc"])

    for c in range(NT):
        bkc = 4 + (c % 2)
        pd = (G.p4 if c % 2 == 0 else G.p5)[:, 0:64]
        for kc in range(KC):
            MM(P, pd, G.uT[:, kc, c * 128:(c + 1) * 128], S.wdt[:, kc, :], kc == 0, kc == KC - 1,
               [("uT", c), lt + "wdt"], [], bk=[bkc])
        TT(P, S.tA[:, c, :], pd, S.dtb[:], ALU.add, [lt + "dtb"], [(lt + "tA", c)], bk=[bkc])
    kA, kB, kl, ka = lt + "tA", "sq", lt + "lndt", lt + "a_all"
    allA = [(lt + "tA", c) for c in range(NT)]
    TS(P, S.tB[:], S.tA[:], -1.0, None, ALU.mult, None, allA, [kB])
    TT(P, S.tB[:], S.tB[:], S.tA[:], ALU.min, allA + [kB], [kB])
    ACT(P, S.tB[:], S.tB[:], AF.Exp, [kB], [kB])
    TS(P, S.tB[:], S.tB[:], 1.0, None, ALU.add, None, [kB], [kB])
    ACT(P, S.tB[:], S.tB[:], AF.Ln, [kB], [kB])
    TS(P, S.tA[:], S.tA[:], 0.0, None, ALU.max, None, allA, allA)
    TT(P, S.tA[:], S.tA[:], S.tB[:], ALU.add, allA + [kB], allA)
    ACT(P, S.lndt[:], S.tA[:], AF.Ln, allA, [kl])
    TT(P, S.a_all[:], S.tA[:], S.Abc[:].unsqueeze(1).to_broadcast([128, NT, 64]), ALU.mult,
       allA + [lt + "Abc"], [ka])

    b4 = lambda ap: ap.unsqueeze(2).to_broadcast([128, 4, 64])
    v64 = lambda ap: ap.rearrange("p (h e) -> p h e", h=4)
    v4 = lambda t: t[:, :, :].rearrange("p c (d h) -> p c d h", d=2)

    for g in range(_NG):
        gt = lt
        DMA(P, "pool", c2(S.wA[:]), c2(D.ssd_wA[j, g]), [], [gt + "wA"])
        DMA(P, "pool", S.wz[:].rearrange("p a b -> p (a b)"), D.ssd_wz[j, g], [], [gt + "wz"])
        DMA(P, "pool", S.wo[:].rearrange("p a b -> p (a b)"), D.ssd_wo[j, g], [], [gt + "wo"])
        DMA(P, "sp", S.gnw[:], D.ssd_norm_w[j:j + 1, g * 256:(g + 1) * 256].partition_broadcast(128),
            [], [gt + "gnw"])
        a_g = S.a_all[:, :, :].rearrange("p c (d g h) -> p c d g h", d=2, g=8)[:, :, :, g, :]
        l_g = S.lndt[:, :, :].rearrange("p c (d g h) -> p c d g h", d=2, g=8)[:, :, :, g, :]
        CP(P, v4(S.ahi), a_g, [ka], [gt + "ahi"])
        TT(P, v4(S.alo), a_g, v4(S.ahi), ALU.subtract, [ka, gt + "ahi"], [gt + "alo"])
        pcs = G.p5[:, 0:128].rearrange("p (c k) -> p c k", k=8)
        ptot = G.p5[:, 128:256].rearrange("p (c k) -> p c k", k=8)
        for c in range(NT):
            for (dst, lo, hi, ci) in ((pcs, 0, 4, C_UI), (pcs, 4, 8, C_UIT), (ptot, 0, 8, C_ONE)):
                MM(P, dst[:, c, lo:hi], cst[:, ci, :], S.ahi[:, c, lo:hi], True, False,
                   [gt + "ahi", "cst"], [], bk=[5])
                MM(P, dst[:, c, lo:hi], cst[:, ci, :], S.alo[:, c, lo:hi], False, True,
                   [gt + "alo", "cst"], [], bk=[5])
        TT(P, v4(S.nb), l_g, v4(pcs), ALU.subtract, [kl], [gt + "nb"], bk=[5])
        ACT(P, S.ecs[:], pcs, AF.Exp, [], [gt + "ecs"], bk=[5])
        ACT(P, S.edec[:], ptot, AF.Exp, [], [gt + "edec"], bk=[5])
        TT(P, S.tg[:], ptot, S.nb[:], ALU.add, [gt + "nb"], [gt + "tg"], bk=[5])
        ACT(P, S.dsc[:], S.tg[:], AF.Exp, [gt + "tg"], [gt + "dsc"])

        tiles = [(S.wx[:, :, 0:128], gt + "wA", g * 2, S.xcT, "x", 0),
                 (S.wx[:, :, 128:256], gt + "wA", g * 2 + 1, S.xcT, "x", 1),
                 (S.wB, gt + "wA", 16 + g, S.BT, "B", 0),
                 (S.wC, gt + "wA", 24 + g, S.CT, "C", 0)]
        for (wt, wk, ct, dst, kind, sub) in tiles:
            for tb in range(4):
                for kc in range(KC):
                    MM(P, pbank[tb], wt[:, kc, :], G.uT[:, kc, tb * 512:(tb + 1) * 512], kc == 0, kc == KC - 1,
                       uT_keys(tb) + [wk], [], bk=[tb])
            for b_ in range(4):
                ACT(P, S.acc[:, b_ * 512:(b_ + 1) * 512], pbank[b_], AF.Identity,
                    [lt + "convw", lt + "convb"], [gt + "acc"],
                    bias=S.convb[:, ct:ct + 1], scale=S.convw[:, ct, 2:3], bk=[b_])
            for tap in (1, 3, 0, 4):
                s = tap - 2
                t0, t1 = max(0, -s), SEQ - max(0, s)
                STT(P, S.acc[:, t0:t1], G.pbig[:, t0 + s:t1 + s], S.convw[:, ct, tap:tap + 1], S.acc[:, t0:t1],
                    ALU.mult, ALU.add, [gt + "acc", lt + "convw"], [gt + "acc"], bk=ALLB)
            dk = gt + "T" + kind
            ACT(P, dst[:], S.acc[:], AF.Silu, [gt + "acc"], [dk])
            if kind == "C":
                continue
            for half in range(2):
                pt = G.p7[:] if half == 0 else p6b
                pbk = [7] if half == 0 else [6]
                for ii in range(8):
                    i = half * 8 + ii
                    TR(P, pt[:, ii * 128:(ii + 1) * 128], dst[:, i * 128:(i + 1) * 128], ident, [dk, "cst"], [], bk=pbk)
                src = pt.rearrange("p (i c) -> p i c", i=8)
                if kind == "x":
                    CP(P, S.xt[:, half * 8:half * 8 + 8, sub * 128:(sub + 1) * 128], src, [],
                       [(gt + "xt", i) for i in range(half * 8, half * 8 + 8)],
                       eng=("act" if half == 0 else "dve"), bk=pbk)
                else:
                    CP(P, S.Bt[:, half * 8:half * 8 + 8, :], src, [],
                       [(gt + "Bt", i) for i in range(half * 8, half * 8 + 8)],
                       eng=("act" if half == 0 else "dve"), bk=pbk)

        Ss = S.acc[:].bitcast(BF16).rearrange("p (c e) -> p c e", c=NT)
        ak = gt + "acc"
        P.op("dve", lambda e: e.memset(S.GS[:], 0.0), [], [gt + "GS"])
        P.op("dve", lambda e: e.memset(S.SS[:], 0.0), [], [gt + "SS"])
        P.op("dve", lambda e: e.memset(S.Gs[:, NT - 1, :], 0.0), [], [(gt + "Gs", NT - 1)])
        P.op("dve", lambda e: e.memset(Ss[:, 0, :], 0.0), [], [ak])

        def st_mm(c, d, i):
            xd = S.xdd[d][i % 2]
            xk = (gt + "xdd", d, i % 2)
            bkn = (4 + (i % 2)) if d == 0 else (6 + (i % 2))
            bank = [G.p4[:], G.p5[:], G.p6[:], p7f][bkn - 4]
            TT(P, v64(xd[:]), v64(S.xt[:, c, :]), b4(S.dsc[:, c, d * 4:d * 4 + 4]), ALU.mult,
               [(gt + "xt", c), gt + "dsc"], [xk], eng=("dve" if d == 0 else "pool"))
            MM(P, bank[:, 0:256], S.Bt[:, c, :], xd[:], True, True, [(gt + "Bt", c), xk], [], bk=[bkn])
            return bank[:, 0:256], bkn

        pend = {}
        pend[(0, 0)] = st_mm(0, 0, 0)
        pend[(1, 0)] = st_mm(NT - 1, 1, 0)
        for i in range(NT - 1):
            cf, cb = i, NT - 1 - i
            if i + 1 < NT - 1:
                pend[(0, i + 1)] = st_mm(cf + 1, 0, i + 1)
                pend[(1, i + 1)] = st_mm(cb - 1, 1, i + 1)
            pst, bkn = pend.pop((0, i))
            TT(P, v64(S.SS[:]), v64(S.SS[:]), b4(S.edec[:, cf, 0:4]), ALU.mult, [gt + "SS", gt + "edec"], [gt + "SS"])
            TT(P, S.SS[:], pst, S.SS[:], ALU.add, [gt + "SS"], [gt + "SS"], bk=[bkn])
            CP(P, Ss[:, cf + 1, :], S.SS[:], [gt + "SS"], [ak], eng="act")
            pst, bkn = pend.pop((1, i))
            TT(P, v64(S.GS[:]), v64(S.GS[:]), b4(S.edec[:, cb, 4:8]), ALU.mult, [gt + "GS", gt + "edec"], [gt + "GS"])
            TT(P, S.GS[:], pst, S.GS[:], ALU.add, [gt + "GS"], [gt + "GS"], bk=[bkn])
            CP(P, S.Gs[:, cb - 1, :], S.GS[:], [gt + "GS"], [(gt + "Gs", cb - 1)], eng="act")

        def tail(c):
            par = c % 2
            yn, ynT = S.yn[par], S.ynT[par]
            ptr = G.p7[:, 0:256]
            for jj in range(2):
                TR(P, ptr[:, jj * 128:(jj + 1) * 128], yn[:, jj * 128:(jj + 1) * 128], ident,
                   [(gt + "yn", par), "cst"], [], bk=[7])
            CP(P, ynT[:], ptr.rearrange("p (j t) -> p j t", j=2), [], [(gt + "ynT", par)], eng="act", bk=[7])
            for ch in range(2):
                po = G.p6[:] if ch == 0 else p7f
                pok = [6] if ch == 0 else [7]
                for jj in range(2):
                    MM(P, po, ynT[:, jj, :], S.wo[:, jj, ch * 512:(ch + 1) * 512], jj == 0, jj == 1,
                       [(gt + "ynT", par), gt + "wo"], [], bk=pok)
                hs = G.h[:, c, ch * 512:(ch + 1) * 512]
                TT(P, hs, po, hs, ALU.add, [("h", c)], [("h", c)], bk=pok)

        for c in range(NT + 4):
          if c < NT:
            par = c % 2
            tok = slice(c * 128, (c + 1) * 128)
            pcb = pbank[2][:, 0:128]
            MM(P, pcb, S.BT[:, tok], S.CT[:, tok], True, True, [gt + "TB", gt + "TC"], [], bk=[2])
            pyo = G.p4
            MM(P, pyo[:, 0:256], S.CT[:, tok], Ss[:, c, :], True, True, [gt + "TC", ak], [], bk=[4])
            MM(P, pyo[:, 256:512], S.CT[:, tok], S.Gs[:, c, :], True, True, [gt + "TC", (gt + "Gs", c)], [], bk=[4])
            pz = G.p5[:, 0:256]
            for kc in range(KC):
                MM(P, pz, G.uT[:, kc, tok], S.wz[:, kc, :], kc == 0, kc == KC - 1, [("uT", c), gt + "wz"], [], bk=[5])
            sz = S.sz[par]
            ACT(P, sz[:], pz, AF.Tanh, [], [(gt + "sz", par)], scale=0.5, bk=[5])
            STT(P, sz[:], sz[:], 1.0, pz, ALU.add, ALU.mult, [(gt + "sz", par)], [(gt + "sz", par)], bk=[5])
            py = pbank[3][:, 0:256]
            its = [(hh, d) for hh in range(4) for d in range(2)]

            def lps(n):
                hh, d = its[n]
                col = d * 4 + hh
                lb = n % 2
                pl = pbank[lb][:, 0:128]
                ucst = cst[:, C_UI if d == 0 else C_UIT, :]
                mcst = cst[:, C_NMF if d == 0 else C_NMB, :]
                MM(P, pl, S.ahi[:, c, col:col + 1].to_broadcast([128, 128]), ucst, True, False,
                   [gt + "ahi", "cst"], [], bk=[lb])
                MM(P, pl, S.alo[:, c, col:col + 1].to_broadcast([128, 128]), ucst, False, False,
                   [gt + "alo", "cst"], [], bk=[lb])
                MM(P, pl, ident, mcst, False, True, ["cst"], [], bk=[lb])

            lps(0)
            lps(1)
            for n in range(8):
                hh, d = its[n]
                col = d * 4 + hh
                sl = n % 4
                lb = n % 2
                pl = pbank[lb][:, 0:128]
                Lx = S.Lexp[sl]
                ACT(P, Lx[:], pl, AF.Exp, [gt + "nb"], [(gt + "Lexp", sl)], bias=S.nb[:, c, col:col + 1], bk=[lb])
                MTt = S.MT[sl]
                TT(P, MTt[:], pcb, Lx[:], ALU.mult, [(gt + "Lexp", sl)], [(gt + "MT", sl)], bk=[2])
                if n + 2 < 8:
                    lps(n + 2)
                MM(P, py[:, hh * 64:(hh + 1) * 64], MTt[:], S.xt[:, c, hh * 64:(hh + 1) * 64], d == 0, d == 1,
                   [(gt + "MT", sl), (gt + "xt", c)], [], bk=[3])
            t1, t2, t3, y = S.t1[par], S.t2[par], S.t3[par], S.y[c % 4]
            k1, k2, k3, ky = (gt + "t1", par), (gt + "t2", par), (gt + "t3", par), (gt + "y", c % 4)
            TT(P, v64(t1[:]), v64(pyo[:, 0:256]), b4(S.ecs[:, c, 0:4]), ALU.mult, [gt + "ecs"], [k1], bk=[4])
            TT(P, v64(t2[:]), v64(pyo[:, 256:512]), b4(S.ecs[:, c, 4:8]), ALU.mult, [gt + "ecs"], [k2], bk=[4])
            TT(P, y[:], py, t1[:], ALU.add, [k1], [ky], bk=[3])
            TT(P, v64(t3[:]), v64(S.xt[:, c, :]), b4(S.Dt[:, g * 4:g * 4 + 4]), ALU.mult,
               [(gt + "xt", c), lt + "Dt"], [k3], eng="pool")
            TT(P, t2[:], t2[:], t3[:], ALU.add, [k2, k3], [k2], eng="pool")
            TT(P, y[:], y[:], t2[:], ALU.add, [ky, k2], [ky], eng="pool")
            TT(P, y[:], y[:], sz[:], ALU.mult, [ky, (gt + "sz", par)], [ky], eng="pool")
            TT(P, S.sqy[par][:], y[:], y[:], ALU.mult, [ky], [(gt + "sqy", par)], eng="pool")
          if 1 <= c <= NT:
            cc = c - 1
            pi, col = (cc // 2) % 4, cc % 2
            sst = S.ssq[pi]
            ssq = sst[:, col:col + 1]
            sk = (gt + "ssq", pi, col)
            P.op("dve", lambda e, ssq=ssq, q=S.sqy[cc % 2]: e.tensor_reduce(out=ssq, in_=q[:], axis=AX.X, op=ALU.add),
                 [(gt + "sqy", cc % 2)], [sk])
            TS(P, ssq, ssq, 1.0 / 256.0, 4.0 * EPS, ALU.mult, ALU.add, [sk], [sk])
            if col == 1:
                sks = [(gt + "ssq", pi, 0), (gt + "ssq", pi, 1)]
                ACT(P, sst[:], sst[:], AF.Sqrt, sks, sks)
          if 3 <= c <= NT + 2:
            cc = c - 3
            pi, col = (cc // 2) % 4, cc % 2
            ssq = S.ssq[pi][:, col:col + 1]
            sk = (gt + "ssq", pi, col)
            P.op("dve", lambda e, ssq=ssq: e.reciprocal(out=ssq, in_=ssq), [sk], [sk])
            yn = S.yn[cc % 2]
            STT(P, yn[:], S.y[cc % 4][:], ssq, S.gnw[:], ALU.mult, ALU.mult,
                [(gt + "y", cc % 4), sk, gt + "gnw"], [(gt + "yn", cc % 2)])
          if c >= 4:
            tail(c - 4)


_CACHE = {}


def _host_inputs(inp):
    f32 = np.float32
    bands_np, _ = _pool_bands()
    convw = np.asarray(inp["ssd_conv_w"], f32)
    convw_l = np.ascontiguousarray(convw.reshape(2, 5, 32, 128).transpose(0, 3, 2, 1)).reshape(2, 128, 160)
    convb_l = np.ascontiguousarray(np.asarray(inp["ssd_conv_b"], f32).reshape(2, 32, 128).transpose(0, 2, 1))
    pscale_l = np.ascontiguousarray(np.asarray(inp["pool_scale"], f32).reshape(2, 16, 128).transpose(0, 2, 1))
    W = np.asarray(inp["ssd_w_in"], f32).reshape(2, KC, 128, 6208)
    wxp = W[..., 2048:4096].reshape(2, KC, 128, 8, 256).transpose(0, 3, 2, 1, 4).reshape(2, 8, 128, 2048)
    wBp = W[..., 4096:5120].reshape(2, KC, 128, 8, 128).transpose(0, 3, 2, 1, 4).reshape(2, 8, 128, 1024)
    wCp = W[..., 5120:6144].reshape(2, KC, 128, 8, 128).transpose(0, 3, 2, 1, 4).reshape(2, 8, 128, 1024)
    wA = np.ascontiguousarray(np.concatenate([wxp, wBp, wCp], axis=-1))
    wz = np.ascontiguousarray(W[..., 0:2048].reshape(2, KC, 128, 8, 256).transpose(0, 3, 2, 1, 4).reshape(2, 8, 128, 2048))
    wdt = np.ascontiguousarray(W[..., 6144:6208].transpose(0, 2, 1, 3).reshape(2, 128, 512))
    wso = np.ascontiguousarray(np.asarray(inp["ssd_w_out"], f32).reshape(2, 8, 2, 128, 1024)
                               .transpose(0, 1, 3, 2, 4).reshape(2, 8, 128, 2048))
    PW = np.asarray(inp["pool_w_in"], f32).reshape(2, KC, 128, 2, 4, 512)
    pwvg = np.ascontiguousarray(PW.transpose(0, 4, 3, 2, 1, 5).reshape(2, 4, 2, 128, 4096))
    pmix = np.ascontiguousarray(np.asarray(inp["pool_mix_w"], f32).reshape(2, 4, 4, 128, 512)
                                .transpose(0, 1, 3, 2, 4).reshape(2, 4, 128, 2048))
    pwo = np.ascontiguousarray(np.asarray(inp["pool_w_out"], f32).reshape(2, 4, 4, 128, 1024)
                               .transpose(0, 1, 3, 2, 4).reshape(2, 4, 128, 4096))
    shared = {
        "norm_w": np.ascontiguousarray(inp["norm_w"], f32),
        "final_norm_w": np.ascontiguousarray(inp["final_norm_w"], f32).reshape(1, D_MODEL),
        "ssd_wA": wA, "ssd_wz": wz, "ssd_wo": wso, "ssd_wdt": wdt,
        "ssd_convw": convw_l,
        "ssd_convb": convb_l,
        "ssd_dt_bias": np.ascontiguousarray(inp["ssd_dt_bias"], f32).reshape(2, 64),
        "ssd_a_log": np.ascontiguousarray(inp["ssd_a_log"], f32).reshape(2, 64),
        "ssd_d": np.ascontiguousarray(inp["ssd_d"], f32),
        "ssd_norm_w": np.ascontiguousarray(inp["ssd_norm_w"], f32),
        "pool_wvg": pwvg, "pool_mix": pmix, "pool_wo": pwo,
        "pool_scale": pscale_l,
        "cst": _ssd_consts().reshape(128, 6 * 128),
        "bands": bands_np.reshape(128, -1),
    }
    return shared


def run_layers(inp, x, layers, final_norm):
    key = (tuple(layers), final_norm)
    if key not in _CACHE:
        _CACHE[key] = build_program(layers, final_norm)[0]
    nc = _CACHE[key]
    shared = _host_inputs(inp)
    B = x.shape[0]
    in_maps = []
    for b in range(B):
        m = dict(shared)
        m["x"] = np.ascontiguousarray(x[b], np.float32)
        in_maps.append(m)
    res = run_bass_kernel_spmd(nc, in_maps, core_ids=list(range(B)))
    return np.stack([np.asarray(r["out"], np.float32) for r in res.results], axis=0)


def kernel(**inputs):
    x = np.asarray(inputs["x"], np.float32)
    return run_layers(inputs, x, (0, 1, 2, 3), True)
```
